# Optimizing a Trainium2 kernel written in Bass

```python
import jax, jax.numpy as jnp
from jax import lax
import numpy as np

D_MODEL = 2048
BATCH = 1
SEQ = 8192
DEPTH = 1
DEC_BATCH = 4
DEC_SEQ = 2048
PAST_LEN = 128

GRID_W = 64
NA_HEADS = 8
NA_HEAD_DIM = 128
NA_WIDTH = NA_HEADS * NA_HEAD_DIM
NA_WIN_ROWS = 8
NA_WIN_COLS = 16
ML_HEADS = 4
ML_HEAD_DIM = 256
ML_WIDTH = ML_HEADS * ML_HEAD_DIM
ML_CHUNK = 64
ML_CONV_W = 5
MIX_WIDTH = NA_WIDTH + ML_WIDTH
N_GATE_COLS = 2 * 2 * ML_HEADS
PROJ_COLS = 4 * NA_WIDTH + 5 * ML_WIDTH + N_GATE_COLS
RMS_EPS = 1e-6

kernel_name = 'hymba_na2d_mlstm_bidir_encoder'


def rmsnorm(x, w):
    xf = x.astype(jnp.float32)
    y = xf * lax.rsqrt(jnp.mean(xf * xf, axis=-1, keepdims=True) + RMS_EPS)
    return (y * w.astype(jnp.float32)).astype(x.dtype)


def window_start(i, win, n):
    return jnp.clip(i - win // 2, 0, n - win)


def neighbourhood_attention(q, k, v, rpb):
    B, T, H, Dh = q.shape
    rows = T // GRID_W
    wr = min(NA_WIN_ROWS, rows)
    wc = NA_WIN_COLS
    qg = q.reshape(B, rows, GRID_W, H, Dh)
    kg = k.reshape(B, rows, GRID_W, H, Dh)
    vg = v.reshape(B, rows, GRID_W, H, Dh)
    cols = jnp.arange(GRID_W)
    col_idx = window_start(cols, wc, GRID_W)[:, None] + jnp.arange(wc)[None, :]
    col_off = col_idx - cols[:, None] + (NA_WIN_COLS - 1)
    rpb_cols = rpb[:, :, col_off]
    scale = Dh ** -0.5

    def row_block(r):
        rs = window_start(r, wr, rows)
        kb = lax.dynamic_slice_in_dim(kg, rs, wr, axis=1)
        vb = lax.dynamic_slice_in_dim(vg, rs, wr, axis=1)
        kw = kb[:, :, col_idx]
        vw = vb[:, :, col_idx]
        qr = lax.dynamic_index_in_dim(qg, r, axis=1, keepdims=False)
        row_off = rs + jnp.arange(wr) - r + (NA_WIN_ROWS - 1)
        bias = jnp.take(rpb_cols, row_off, axis=1)
        s = jnp.einsum('bchd,bicjhd->bhcij', qr, kw, preferred_element_type=jnp.float32) * scale
        s = s + jnp.transpose(bias, (0, 2, 1, 3))[None].astype(jnp.float32)
        p = jax.nn.softmax(s.reshape(B, H, GRID_W, wr * wc), axis=-1).reshape(s.shape).astype(v.dtype)
        return jnp.einsum('bhcij,bicjhd->bchd', p, vw)

    out = lax.map(row_block, jnp.arange(rows))
    return jnp.transpose(out, (1, 0, 2, 3, 4)).reshape(B, T, H * Dh)


def mlstm_one_direction(q, k, v, log_i, log_f):
    B, T, H, D = q.shape
    L = ML_CHUNK
    NC = T // L
    qc = q.reshape(B, NC, L, H, D)
    kc = k.reshape(B, NC, L, H, D)
    vc = v.reshape(B, NC, L, H, D)
    ic = log_i.reshape(B, NC, L, H)
    fc = log_f.reshape(B, NC, L, H)
    b = jnp.cumsum(fc, axis=2)
    g = b[:, :, -1]
    a = g[:, :, None] - b + ic
    m_loc = jnp.max(a, axis=2)
    kw = kc * jnp.exp(a - m_loc[:, :, None])[..., None]
    kv_chunk = jnp.einsum('bnlhd,bnlhe->bnhde', kw, vc)
    k_chunk = jnp.sum(kw, axis=2)

    def step(carry, inp):
        C, n, m = carry
        g_c, m_c, kv_c, k_c = inp
        m_new = jnp.maximum(g_c + m, m_c)
        s_old = jnp.exp(g_c + m - m_new)
        s_cur = jnp.exp(m_c - m_new)
        C_new = s_old[..., None, None] * C + s_cur[..., None, None] * kv_c
        n_new = s_old[..., None] * n + s_cur[..., None] * k_c
        return (C_new, n_new, m_new), (C, n, m)

    init = (jnp.zeros((B, H, D, D), q.dtype), jnp.zeros((B, H, D), q.dtype), jnp.zeros((B, H), q.dtype))
    xs = (jnp.moveaxis(g, 1, 0), jnp.moveaxis(m_loc, 1, 0), jnp.moveaxis(kv_chunk, 1, 0), jnp.moveaxis(k_chunk, 1, 0))
    _, (C_prev, n_prev, m_prev) = lax.scan(step, init, xs)
    C_prev = jnp.moveaxis(C_prev, 0, 1)
    n_prev = jnp.moveaxis(n_prev, 0, 1)
    m_prev = jnp.moveaxis(m_prev, 0, 1)

    bh = jnp.swapaxes(b, 2, 3)
    ih = jnp.swapaxes(ic, 2, 3)
    lower = jnp.tril(jnp.ones((L, L), dtype=bool))
    d_log = jnp.where(lower, bh[..., :, None] - bh[..., None, :] + ih[..., None, :], -jnp.inf)
    inter = bh + m_prev[..., None]
    m_row = jnp.maximum(inter, jnp.max(d_log, axis=-1))
    p = jnp.exp(d_log - m_row[..., None]) * jnp.einsum('bnlhd,bnshd->bnhls', qc, kc)
    e_inter = jnp.exp(inter - m_row)
    num = e_inter[..., None] * jnp.einsum('bnlhd,bnhde->bnhle', qc, C_prev) + jnp.einsum('bnhls,bnshe->bnhle', p, vc)
    den = e_inter * jnp.einsum('bnlhd,bnhd->bnhl', qc, n_prev) + jnp.sum(p, axis=-1)
    h = num / jnp.maximum(jnp.abs(den), jnp.exp(-m_row))[..., None]
    return jnp.swapaxes(h, 2, 3).reshape(B, T, H, D)


def bidirectional_mlstm(q, k, v, gates):
    log_i = gates[:, :, :, 0]
    log_f = jax.nn.log_sigmoid(gates[:, :, :, 1])
    h_fwd = mlstm_one_direction(q, k, v, log_i[:, :, 0], log_f[:, :, 0])
    rev = lambda t: jnp.flip(t, axis=1)
    h_bwd = rev(mlstm_one_direction(rev(q), rev(k), rev(v), rev(log_i[:, :, 1]), rev(log_f[:, :, 1])))
    return h_fwd + h_bwd


def centred_depthwise_conv(x, w, b):
    C = x.shape[-1]
    pad = ML_CONV_W // 2
    y = lax.conv_general_dilated(x, w[:, None, :].astype(x.dtype), window_strides=(1,), padding=[(pad, pad)],
                                 dimension_numbers=('NWC', 'WIO', 'NWC'), feature_group_count=C)
    return y + b.astype(x.dtype)


def hybrid_layer(x, norm_w, w_in, na_rpb, conv_w, conv_b, gate_b, head_norm_w, w_out):
    B, T, _ = x.shape
    f32 = jnp.float32
    hn = rmsnorm(x, norm_w)
    proj = jnp.einsum('btd,dp->btp', hn, w_in)
    cuts = [int(c) for c in np.cumsum([NA_WIDTH] * 4 + [2 * ML_WIDTH, ML_WIDTH, ML_WIDTH, ML_WIDTH])]
    na_q, na_k, na_v, na_g, ml_qk, ml_v, ml_o, ml_z, ml_gp = jnp.split(proj, cuts, axis=-1)
    att = neighbourhood_attention(na_q.reshape(B, T, NA_HEADS, NA_HEAD_DIM), na_k.reshape(B, T, NA_HEADS, NA_HEAD_DIM),
                                  na_v.reshape(B, T, NA_HEADS, NA_HEAD_DIM), na_rpb)
    att = (att * jax.nn.silu(na_g)).astype(x.dtype)
    qk = jax.nn.silu(centred_depthwise_conv(ml_qk, conv_w, conv_b))
    mq, mk = jnp.split(qk, 2, axis=-1)
    mq = mq.reshape(B, T, ML_HEADS, ML_HEAD_DIM).astype(f32)
    mk = mk.reshape(B, T, ML_HEADS, ML_HEAD_DIM).astype(f32) * (ML_HEAD_DIM ** -0.5)
    mv = ml_v.reshape(B, T, ML_HEADS, ML_HEAD_DIM).astype(f32)
    gates = ml_gp.astype(f32).reshape(B, T, 2, 2, ML_HEADS) + gate_b.astype(f32)
    h = bidirectional_mlstm(mq, mk, mv, gates)
    h = rmsnorm(h, head_norm_w.reshape(ML_HEADS, ML_HEAD_DIM)).reshape(B, T, ML_WIDTH)
    mem = (h * jax.nn.sigmoid(ml_o.astype(f32)) * jax.nn.silu(ml_z.astype(f32))).astype(x.dtype)
    mixed = jnp.concatenate([att, mem], axis=-1)
    return x + jnp.einsum('btm,md->btd', mixed, w_out)


def setup_inputs(seed: int = 0) -> dict:
    key = jax.random.key(seed)
    ks = jax.random.split(key, 12)
    nrm = jax.random.normal
    x_prompt = nrm(ks[0], (BATCH, SEQ, D_MODEL), jnp.float32)
    x_sample = nrm(ks[1], (DEC_BATCH, DEC_SEQ, D_MODEL), jnp.float32)
    ln_w = 1.0 + 0.02 * nrm(ks[2], (DEPTH, D_MODEL), jnp.float32)
    w_in = nrm(ks[3], (DEPTH, D_MODEL, PROJ_COLS), jnp.float32) * (D_MODEL ** -0.5)
    na_rpb = 0.1 * nrm(ks[4], (DEPTH, NA_HEADS, 2 * NA_WIN_ROWS - 1, 2 * NA_WIN_COLS - 1), jnp.float32)
    ml_conv_w = nrm(ks[5], (DEPTH, ML_CONV_W, 2 * ML_WIDTH), jnp.float32) * (ML_CONV_W ** -0.5)
    ml_conv_b = 0.02 * nrm(ks[6], (DEPTH, 2 * ML_WIDTH), jnp.float32)
    ig_b = 0.1 * nrm(ks[7], (DEPTH, 2, 1, ML_HEADS), jnp.float32)
    fg_b = jnp.linspace(3.0, 6.0, ML_HEADS, dtype=jnp.float32)[None, None, None, :] + 0.1 * nrm(ks[8], (DEPTH, 2, 1, ML_HEADS), jnp.float32)
    ml_gate_b = jnp.concatenate([ig_b, fg_b], axis=2)
    ml_norm_w = 1.0 + 0.02 * nrm(ks[9], (DEPTH, ML_WIDTH), jnp.float32)
    w_out = nrm(ks[10], (DEPTH, MIX_WIDTH, D_MODEL), jnp.float32) * (MIX_WIDTH ** -0.5)
    final_norm_w = 1.0 + 0.02 * nrm(ks[11], (D_MODEL,), jnp.float32)
    return {'x_prompt': x_prompt, 'x_sample': x_sample, 'ln_w': ln_w, 'w_in': w_in, 'na_rpb': na_rpb,
            'ml_conv_w': ml_conv_w, 'ml_conv_b': ml_conv_b, 'ml_gate_b': ml_gate_b, 'ml_norm_w': ml_norm_w,
            'w_out': w_out, 'final_norm_w': final_norm_w}


def reference(x_prompt, x_sample, ln_w, w_in, na_rpb, ml_conv_w, ml_conv_b, ml_gate_b, ml_norm_w, w_out, final_norm_w):
    def trunk(x):
        for l in range(DEPTH):
            x = hybrid_layer(x, ln_w[l], w_in[l], na_rpb[l], ml_conv_w[l], ml_conv_b[l], ml_gate_b[l], ml_norm_w[l], w_out[l])
        return rmsnorm(x, final_norm_w)
    y_prompt = trunk(x_prompt)
    y_sample = trunk(x_sample)
    return (y_prompt, y_sample)
```

```python
import contextlib
import numpy as np
import concourse.bass as bass
import concourse.mybir as mybir
from concourse.bass_utils import run_bass_kernel_spmd

F32 = mybir.dt.float32
BF16 = mybir.dt.bfloat16
AF = mybir.ActivationFunctionType
ALU = mybir.AluOpType

ENGS = ("pe", "act", "dve", "pool", "sp")
NDSEM = 8

D = 2048
NTOK = 2048
NEG = -30000.0
EPS = 1e-6
NGRP = 73
SLOT = 4232
SPECIAL = [(0, 0), (0, 1), (0, 4), (0, 5), (1, 1), (1, 5), (14, 14), (14, 18), (15, 14), (15, 15), (15, 18), (15, 19)]


class _Op:
    __slots__ = ("eng", "fn", "deps", "is_dma", "idx", "signal", "cnt", "dq", "dqi", "cc")

    def __init__(self, eng, fn, is_dma):
        self.eng = eng
        self.fn = fn
        self.deps = []
        self.is_dma = is_dma
        self.signal = False
        self.cnt = 0
        self.dq = None
        self.dqi = 0
        self.cc = None


class Prog:
    def __init__(self, nc):
        self.nc = nc
        self.ops = []
        self.last_w = {}
        self.readers = {}
        self.ndma = {e: 0 for e in ENGS}
        self.ncc = 0
        self.marks = []

    def mark(self, name):
        self.marks.append((name, len(self.ops)))

    def _add(self, op, reads, writes):
        deps = set()
        for t in list(reads) + list(writes):
            w = self.last_w.get(t)
            if w is not None:
                deps.add(w)
        for t in writes:
            for r in self.readers.get(t, ()):
                deps.add(r)
        deps.discard(op)
        op.deps = [d for d in deps
                   if not (d.eng == "pe" and op.eng == "pe" and not d.is_dma and not op.is_dma)]
        for d in op.deps:
            d.signal = True
        for t in writes:
            self.last_w[t] = op
            self.readers[t] = []
        for t in reads:
            self.readers.setdefault(t, []).append(op)
        op.idx = len(self.ops)
        self.ops.append(op)
        return op

    def op(self, eng, fn, reads=(), writes=()):
        return self._add(_Op(eng, fn, False), reads, writes)

    def wait_all(self, eng, reads):
        return self._add(_Op(eng, None, False), reads, ())

    def dma(self, eng, out, in_, reads=(), writes=(), **kw):
        return self.dma_fn(eng, lambda e: e.dma_start(out=out, in_=in_, **kw), reads, writes)

    def dma_fn(self, eng, fn, reads=(), writes=()):
        o = _Op(eng, fn, True)
        o.dq = eng
        o.dqi = self.ndma[eng]
        self.ndma[eng] += 1
        o.signal = True
        return self._add(o, reads, writes)

    def cc_fn(self, eng, fn, reads=(), writes=()):
        o = _Op(eng, fn, False)
        o.cc = self.ncc
        self.ncc += 1
        return self._add(o, reads, writes)

    def barrier(self, skip=()):
        deps = []
        for e in ENGS:
            for o in reversed(self.ops):
                if o.eng == e and not o.is_dma and o.fn is not None and o.cc is None:
                    deps.append(o)
                    break
        cnt = {}
        for o in reversed(self.ops):
            if o.is_dma and cnt.get(o.dq, 0) < NDSEM:
                cnt[o.dq] = cnt.get(o.dq, 0) + 1
                if o not in skip:
                    deps.append(o)
        for d in deps:
            d.signal = True
        for e in ENGS:
            w = _Op(e, None, False)
            w.deps = list(deps)
            w.idx = len(self.ops)
            self.ops.append(w)

    def emit(self):
        nc = self.nc
        with contextlib.ExitStack() as es:
            csem = {e: es.enter_context(nc.semaphore("c_" + e)) for e in ENGS}
            dsem = {e: [es.enter_context(nc.semaphore("d_%s%d" % (e, i))) for i in range(NDSEM)]
                    for e in ENGS if self.ndma[e] > 0}
            ccsem = [es.enter_context(nc.semaphore("cc%d" % i)) for i in range(self.ncc)]
            cc = {e: 0 for e in ENGS}
            for o in self.ops:
                if o.is_dma or o.fn is None or o.cc is not None:
                    continue
                if o.signal:
                    cc[o.eng] += 1
                    o.cnt = cc[o.eng]
            block = es.enter_context(nc.Block())
            ops = self.ops

            def mk(ename):
                def body(eng):
                    waited = {}
                    for o in ops:
                        if o.eng != ename:
                            continue
                        need = {}
                        for d in o.deps:
                            if d.fn is None:
                                continue
                            if d.cc is not None:
                                key = ("cc", d.cc, 0)
                                val = 1
                            elif d.is_dma:
                                key = ("d", d.dq, d.dqi % NDSEM)
                                val = 16 * (d.dqi // NDSEM + 1)
                            else:
                                key = ("c", d.eng)
                                val = d.cnt
                            if need.get(key, 0) < val:
                                need[key] = val
                        if o.is_dma and o.dqi >= NDSEM:
                            key = ("d", o.dq, o.dqi % NDSEM)
                            val = 16 * (o.dqi // NDSEM)
                            if need.get(key, 0) < val:
                                need[key] = val
                        for key, val in need.items():
                            if waited.get(key, 0) >= val:
                                continue
                            waited[key] = val
                            if key[0] == "c":
                                sem = csem[key[1]]
                            elif key[0] == "cc":
                                sem = ccsem[key[1]]
                            else:
                                sem = dsem[key[1]][key[2]]
                            eng.wait_ge(sem, val)
                        if o.fn is None:
                            continue
                        ins = o.fn(eng)
                        if o.cc is not None:
                            ins.then_inc(ccsem[o.cc])
                        elif o.is_dma:
                            ins.then_inc(dsem[o.dq][o.dqi % NDSEM], 16)
                        elif o.signal:
                            ins.then_inc(csem[ename], 1)
                return body

            block.tensor(mk("pe"))
            block.scalar(mk("act"))
            block.vector(mk("dve"))
            block.gpsimd(mk("pool"))
            block.sync(mk("sp"))


class Arena:
    def __init__(self, t, nbytes):
        self.t = t
        self.nbytes = nbytes
        self.off = 0

    def alloc(self, shape, dt):
        esz = 4 if dt == F32 else 2
        n = 1
        for s in shape:
            n *= s
        nb = (n * esz + 31) // 32 * 32
        assert self.off + nb <= self.nbytes, ("arena overflow", self.off, nb, self.nbytes)
        ap = self.t[:, self.off // 4:(self.off + nb) // 4]
        if dt != F32:
            ap = ap.bitcast(dt)
        ap = ap[:, 0:n]
        if len(shape) == 2:
            ap = ap.rearrange("p (a b) -> p a b", a=shape[0])
        elif len(shape) == 3:
            ap = ap.rearrange("p (a b c) -> p a b c", a=shape[0], b=shape[1])
        self.off += nb
        return ap

    def mark(self):
        return self.off

    def release(self, m):
        self.off = m


ARENA_BYTES = 212736


def build_program(stop_after=None):
    nc = bass.Bass("TRN2", target_bir_lowering=False)

    def din(name, shape, dt=F32):
        return nc.dram_tensor(name, shape, dt, kind="ExternalInput").ap()

    xe = din("xe", [2560, D])
    xo_d = din("xo", [3, 2048, D])
    xoh_d = din("xoh", [3, 128, D])
    lnw = din("lnw", [128, D])
    fnw = din("fnw", [128, D])
    w_in = din("w_in", [NGRP, 128, 2048])
    w_out = din("w_out", [128, 16 * 2048])
    convw = din("convw", [128, 16 * 5])
    convb = din("convb", [128, 16])
    gateb = din("gateb", [128, 256])
    mnw = din("mnw", [128, 8])
    ident_d = din("ident", [128, 128])
    tri_d = din("tri", [128, 256])
    bg_d = din("bg", [8, 128, 640])
    bf_d = din("bf", [8, 128, 896])
    msk_d = din("msk", [128, 12 * 128 + 4])
    cf_d = din("cf", [128, 64])
    y_out = nc.dram_tensor("y", [NTOK, D], F32, kind="ExternalOutput").ap()
    mixT = nc.dram_tensor("mixT", [16, 128, 16, 128], BF16).ap()
    cin_s = nc.dram_tensor("cin_s", [8, 128, 528], F32).ap()

    P = Prog(nc)
    with contextlib.ExitStack() as es:
        arena_t = es.enter_context(nc.sbuf_tensor("arena", [128, ARENA_BYTES // 4], F32))
        A = Arena(arena_t, ARENA_BYTES)
        psb = [es.enter_context(nc.psum_tensor("ps%d" % i, [128, 512], F32)) for i in range(8)]

        def PS(i):
            return psb[i][:]

        def PSB(i):
            return psb[i][:].bitcast(BF16)

        def pst(i):
            return ("ps", i)

        ident = A.alloc([128], F32)
        identb = A.alloc([128], BF16)
        tri = A.alloc([256], F32)
        onesb = A.alloc([128], BF16)
        onesf = A.alloc([128], F32)
        cw = A.alloc([16, 5], F32)
        cb = A.alloc([16], F32)
        gb = A.alloc([256], F32)
        mnw_s = A.alloc([8], F32)
        msk = A.alloc([12 * 128 + 4], F32)
        cf = A.alloc([64], F32)
        epsc = A.alloc([8], F32)
        pre_hn_mark = A.mark()
        hnT = A.alloc([16, 2048], BF16)
        hhalo = A.alloc([16, 4], BF16)
        WP = A.alloc([16, 8], F32)
        KH = A.alloc([16, 8], F32)
        ENB = A.alloc([16, 8], F32)
        EG = A.alloc([16, 8], F32)
        WSEG = A.alloc([16, 8], F32)
        GSEG = A.alloc([8], F32)
        GO = A.alloc([3, 8], F32)
        wbufs = [A.alloc([16, 128], BF16) for _ in range(4)]
        pre_halo_mark = A.mark()
        hnTh = A.alloc([16, 512], BF16)
        base_mark = A.mark()

        for nm, dst, src in (("ident", ident, ident_d), ("tri", tri, tri_d), ("cw", cw.rearrange("p a b -> p (a b)"), convw),
                             ("cb", cb, convb), ("gb", gb, gateb), ("mnw", mnw_s, mnw), ("msk", msk, msk_d),
                             ("cf", cf, cf_d)):
            P.dma("sp", dst, src, writes=[nm])
        P.op("dve", lambda e: e.tensor_copy(out=identb, in_=ident), reads=["ident"], writes=["identb"])
        P.op("pool", lambda e: e.memset(onesb, 1.0), writes=["onesb"])
        P.op("pool", lambda e: e.memset(onesf, 1.0), writes=["onesf"])
        P.op("pool", lambda e: e.memset(epsc, EPS), writes=["epsc"])

        wstate = {"n": 0}

        def load_w(g):
            if wstate.get("pre") is not None and wstate["pre"][0] == g:
                i = wstate["pre"][1]
                wstate["pre"] = None
                return i
            assert wstate.get("pre") is None, ("prefetched group not consumed", wstate.get("pre"), g)
            i = wstate["n"] % 4
            wstate["n"] += 1
            P.dma("pool", wbufs[i].rearrange("p a b -> p (a b)"), w_in[g], writes=[("wbuf", i)])
            return i

        def prefetch_w(g):
            i = load_w(g)
            wstate["pre"] = (g, i)

        def proj_mm(wi, rhs_fn, n, bank, ncols=128):
            for kc in range(16):
                P.op("pe", lambda e, kc=kc: e.matmul(out=PS(bank)[0:ncols, 0:n], lhsT=wbufs[wi][:, kc, 0:ncols],
                                                      rhs=rhs_fn(kc), start=(kc == 0), stop=(kc == 15)),
                     reads=[("wbuf", wi), "hnT"], writes=[pst(bank)])

        pbank = {"n": 0}

        def next_bank(lo, cnt):
            b = lo + pbank["n"] % cnt
            pbank["n"] += 1
            return b

        def rhs_main(tt):
            return lambda kc: hnT[:, kc, tt * 512:(tt + 1) * 512]

        def norm_phase(tiles):
            m0 = A.mark()
            lw = A.alloc([D], F32)
            xts = [A.alloc([D], F32) for _ in range(4)]
            junk = A.alloc([D], BF16)
            hns = [A.alloc([D], BF16) for _ in range(2)]
            sss = [A.alloc([2], F32) for _ in range(4)]
            P.dma("sp", lw, lnw, writes=["lw"])
            nt_ = len(tiles)

            def st1(tl):
                i4 = tl % 4
                xt, ss = xts[i4], sss[i4]
                P.dma("sp" if tl % 2 == 0 else "pool", xt, tiles[tl][0], writes=[("xt", i4)])
                P.op("act", lambda e, xt=xt, ss=ss, junk=junk: e.activation(out=junk, in_=xt, func=AF.Square, accum_out=ss[:, 0:1]),
                     reads=[("xt", i4)], writes=["junk", ("ss", i4)])

            def st2_sqrt(tl):
                i4 = tl % 4
                ss = sss[i4]
                P.op("act", lambda e, ss=ss: e.activation(out=ss[:, 1:2], in_=ss[:, 0:1], func=AF.Sqrt, scale=1.0 / D, bias=epsc[:, 0:1]),
                     reads=[("ss", i4), "epsc"], writes=[("ss1", i4)])

            def st2_recip(tl):
                i4 = tl % 4
                ss = sss[i4]
                P.op("dve", lambda e, ss=ss: e.reciprocal(out=ss[:, 1:2], in_=ss[:, 1:2]), reads=[("ss1", i4)], writes=[("ss1", i4)])

            def st3(tl):
                i4, i = tl % 4, tl % 2
                xt, hn, ss = xts[i4], hns[i], sss[i4]
                P.op("dve", lambda e, xt=xt, ss=ss, hn=hn, lw=lw: e.scalar_tensor_tensor(out=hn, in0=xt, scalar=ss[:, 1:2], in1=lw,
                                                                                            op0=ALU.mult, op1=ALU.mult),
                     reads=[("xt", i4), ("ss1", i4), "lw"], writes=[("hn", i)])
                b0 = 4 * i
                for kc in range(16):
                    bank = b0 + kc // 8
                    P.op("pe", lambda e, kc=kc, bank=bank, hn=hn: e.transpose(
                        out=PSB(bank)[:, (kc % 8) * 128:(kc % 8 + 1) * 128], in_=hn[:, kc * 128:(kc + 1) * 128], identity=identb),
                        reads=[("hn", i), "identb"], writes=[pst(bank)])

            def st4(tl, hb):
                i = tl % 2
                b0 = 4 * i
                dst = tiles[tl][1](hb)
                src = PSB(b0 + hb)[:, 0:1024].rearrange("p (a b) -> p a b", a=8)
                if hb == 0:
                    P.op("act", lambda e, dst=dst, src=src: e.copy(out=dst, in_=src), reads=[pst(b0 + hb)], writes=["hnT"])
                else:
                    P.op("dve", lambda e, dst=dst, src=src: e.tensor_copy(out=dst, in_=src), reads=[pst(b0 + hb)], writes=["hnT"])

            for k in range(nt_ + 3):
                if k < nt_:
                    st1(k)
                if 0 <= k - 1 < nt_:
                    st2_sqrt(k - 1)
                if 0 <= k - 2 < nt_:
                    st3(k - 2)
                if 0 <= k - 3 < nt_:
                    st4(k - 3, 0)
                    st4(k - 3, 1)
                if 0 <= k - 1 < nt_:
                    st2_recip(k - 1)
            prefetch_w(72)
            P.barrier()
            A.release(m0)

        def gates_pass(next_group):
            m0 = A.mark()
            gT = A.alloc([2048], F32)
            LI = A.alloc([16, 8], F32)
            LF = A.alloc([16, 8], F32)
            Bc = A.alloc([16, 8], F32)
            Gt = A.alloc([16, 8], F32)
            ROFF = A.alloc([16, 8], F32)
            tmpg = A.alloc([16, 8], F32)
            gtok = A.alloc([16, 16], F32)
            wi = load_w(72)
            for tt in range(4):
                bank = next_bank(0, 4)
                proj_mm(wi, rhs_main(tt), 512, bank, ncols=16)
                P.op("act", lambda e, bank=bank, tt=tt: e.copy(out=gT[0:16, tt * 512:(tt + 1) * 512], in_=PS(bank)[0:16, 0:512]),
                     reads=[pst(bank)], writes=["gT"])
            for tl in range(16):
                P.op("pe", lambda e, tl=tl: e.transpose(out=PS(4)[:, tl * 16:(tl + 1) * 16], in_=gT[0:16, tl * 128:(tl + 1) * 128],
                                                        identity=ident[0:16, 0:16]),
                     reads=["gT", "ident"], writes=[pst(4)])
            P.op("dve", lambda e: e.tensor_tensor(out=gtok.rearrange("p a b -> p (a b)"), in0=PS(4)[:, 0:256], in1=gb, op=ALU.add),
                 reads=[pst(4), "gb"], writes=["gtok"])
            g5 = gtok.rearrange("p n (d t h) -> p n d t h", d=2, t=2)
            for d in range(2):
                P.op("dve", lambda e, d=d: e.tensor_copy(out=LI[:, :, d * 4:(d + 1) * 4], in_=g5[:, :, d, 0, :]),
                     reads=["gtok"], writes=["LI"])
                P.op("act", lambda e, d=d: e.activation(out=LF[:, :, d * 4:(d + 1) * 4], in_=g5[:, :, d, 1, :], func=AF.Exp, scale=-1.0),
                     reads=["gtok"], writes=["LF"])
            P.op("act", lambda e: e.activation(out=LF, in_=LF, func=AF.Ln, bias=1.0), reads=["LF"], writes=["LF"])
            P.op("dve", lambda e: e.tensor_scalar(out=LF, in0=LF, scalar1=-1.0, scalar2=None, op0=ALU.mult), reads=["LF"], writes=["LF"])
            P.op("pe", lambda e: e.matmul(out=PS(5)[:, 0:128], lhsT=tri[:, 0:128], rhs=LF.rearrange("p a b -> p (a b)"),
                                          start=True, stop=True), reads=["LF", "tri"], writes=[pst(5)])
            P.op("pe", lambda e: e.matmul(out=PS(7)[:, 0:128], lhsT=tri[:, 128:256], rhs=LF.rearrange("p a b -> p (a b)"),
                                          start=True, stop=True), reads=["LF", "tri"], writes=[pst(7)])
            P.op("pe", lambda e: e.matmul(out=PS(6)[:, 0:128], lhsT=onesf, rhs=LF.rearrange("p a b -> p (a b)"), start=True, stop=True),
                 reads=["LF", "onesf"], writes=[pst(6)])
            for d in range(2):
                bk = 5 if d == 0 else 7
                P.op("dve", lambda e, d=d, bk=bk: e.tensor_copy(out=Bc[:, :, d * 4:(d + 1) * 4],
                                                                in_=PS(bk)[:, 0:128].rearrange("p (a b) -> p a b", a=16)[:, :, d * 4:(d + 1) * 4]),
                     reads=[pst(bk)], writes=["Bc"])
            P.op("dve", lambda e: e.tensor_copy(out=Gt.rearrange("p a b -> p (a b)"), in_=PS(6)[:, 0:128]), reads=[pst(6)], writes=["Gt"])
            P.op("pool", lambda e: e.memset(ROFF, 0.0), writes=["ROFF"])
            for n in range(14, -1, -1):
                P.op("dve", lambda e, n=n: e.tensor_tensor(out=ROFF[:, n, 0:4], in0=ROFF[:, n + 1, 0:4], in1=Gt[:, n + 1, 0:4], op=ALU.add),
                     reads=["ROFF", "Gt"], writes=["ROFF"])
            for n in range(1, 16):
                P.op("dve", lambda e, n=n: e.tensor_tensor(out=ROFF[:, n, 4:8], in0=ROFF[:, n - 1, 4:8], in1=Gt[:, n - 1, 4:8], op=ALU.add),
                     reads=["ROFF", "Gt"], writes=["ROFF"])
            P.op("dve", lambda e: e.tensor_tensor(out=GSEG[:, 0:4], in0=ROFF[:, 0, 0:4], in1=Gt[:, 0, 0:4], op=ALU.add),
                 reads=["ROFF", "Gt"], writes=["GSEG"])
            P.op("dve", lambda e: e.tensor_tensor(out=GSEG[:, 4:8], in0=ROFF[:, 15, 4:8], in1=Gt[:, 15, 4:8], op=ALU.add),
                 reads=["ROFF", "Gt"], writes=["GSEG"])
            P.op("dve", lambda e: e.tensor_tensor(out=tmpg, in0=LI, in1=Bc, op=ALU.subtract), reads=["LI", "Bc"], writes=["tmpg"])
            P.op("act", lambda e: e.activation(out=WP, in_=tmpg, func=AF.Exp), reads=["tmpg"], writes=["WP"])
            P.op("act", lambda e: e.activation(out=EG, in_=Gt, func=AF.Exp), reads=["Gt"], writes=["EG"])
            P.op("act", lambda e: e.activation(out=ENB, in_=Bc, func=AF.Exp, scale=-1.0), reads=["Bc"], writes=["ENB"])
            P.op("dve", lambda e: e.tensor_tensor(out=KH, in0=WP, in1=EG, op=ALU.mult), reads=["WP", "EG"], writes=["KH"])
            P.op("act", lambda e: e.activation(out=tmpg, in_=ROFF, func=AF.Exp), reads=["ROFF", "WP"], writes=["tmpg"])
            P.op("dve", lambda e: e.tensor_tensor(out=WSEG, in0=KH, in1=tmpg, op=ALU.mult), reads=["KH", "tmpg"], writes=["WSEG"])
            prefetch_w(next_group)
            return m0

        def conv_proj(g, cin, ctok, halo_fn):
            wi = load_w(g)
            for tt in range(4):
                bank = next_bank(0, 4)
                proj_mm(wi, rhs_main(tt), 512, bank)
                P.op("act", lambda e, bank=bank, tt=tt, cin=cin: e.copy(out=cin[:, 2 + tt * 512:2 + (tt + 1) * 512], in_=PS(bank)[:, 0:512]),
                     reads=[pst(bank)], writes=[ctok])
            bank = next_bank(0, 4)
            proj_mm(wi, halo_fn, 4, bank)
            P.op("act", lambda e, bank=bank, cin=cin: e.copy(out=cin[:, 0:2], in_=PS(bank)[:, 0:2]), reads=[pst(bank)], writes=[ctok])
            P.op("act", lambda e, bank=bank, cin=cin: e.copy(out=cin[:, 2050:2052], in_=PS(bank)[:, 2:4]), reads=[pst(bank)], writes=[ctok])

        def conv_post(cgi, dst, dtok, post_scale, cin, ctok, cacc):
            P.op("dve", lambda e, cin=cin, cacc=cacc: e.tensor_scalar(out=cacc, in0=cin[:, 0:2048], scalar1=cw[:, cgi, 0:1], scalar2=None, op0=ALU.mult),
                 reads=[ctok, "cw"], writes=["cacc"])
            for j in range(1, 5):
                P.op("dve", lambda e, j=j, cin=cin, cacc=cacc: e.scalar_tensor_tensor(out=cacc, in0=cin[:, j:j + 2048], scalar=cw[:, cgi, j:j + 1],
                                                                                      in1=cacc, op0=ALU.mult, op1=ALU.add),
                     reads=[ctok, "cw", "cacc"], writes=["cacc"])
            if post_scale is None:
                P.op("act", lambda e, cacc=cacc: e.activation(out=dst, in_=cacc, func=AF.Silu, bias=cb[:, cgi:cgi + 1]),
                     reads=["cacc", "cb"], writes=[dtok])
            else:
                P.op("act", lambda e, cacc=cacc: e.activation(out=cacc, in_=cacc, func=AF.Silu, bias=cb[:, cgi:cgi + 1]),
                     reads=["cacc", "cb"], writes=["cacc"])
                P.op("dve", lambda e, cacc=cacc: e.tensor_scalar(out=dst, in0=cacc, scalar1=post_scale, scalar2=None, op0=ALU.mult),
                     reads=["cacc"], writes=[dtok])

        def transposes_to_tok(srcT, dst_fn, tag, stok="convdst"):
            for n4 in range(4):
                bank = next_bank(4, 4)
                for nn in range(4):
                    n = n4 * 4 + nn
                    for ec in range(2):
                        P.op("pe", lambda e, n=n, nn=nn, ec=ec, bank=bank: e.transpose(
                            out=PSB(bank)[:, nn * 256 + ec * 128: nn * 256 + (ec + 1) * 128],
                            in_=srcT[:, ec, n * 128:(n + 1) * 128], identity=identb),
                            reads=[stok, "identb"], writes=[pst(bank)])
                for nn in range(4):
                    n = n4 * 4 + nn
                    src = PSB(bank)[:, nn * 256:(nn + 1) * 256]
                    dstap = dst_fn(n)
                    if nn % 2 == 0:
                        P.op("act", lambda e, dstap=dstap, src=src: e.copy(out=dstap, in_=src), reads=[pst(bank)], writes=[tag])
                    else:
                        P.op("dve", lambda e, dstap=dstap, src=src: e.tensor_copy(out=dstap, in_=src), reads=[pst(bank)], writes=[tag])

        def kv_operands(h, kTb, ktok, vTb, vaug, cins, cacc, halo_fn, vtok="convdst"):
            def vproj(ec):
                wi = load_w(48 + 2 * h + ec)
                for tt in range(4):
                    bank = next_bank(0, 4)
                    proj_mm(wi, rhs_main(tt), 512, bank)
                    P.op("act", lambda e, bank=bank, tt=tt, ec=ec: e.copy(out=vTb[:, ec, tt * 512:(tt + 1) * 512], in_=PS(bank)[:, 0:512]),
                         reads=[pst(bank)], writes=[vtok])
            conv_proj(40 + 2 * h, cins[0], ("cin", 0), halo_fn)
            conv_proj(41 + 2 * h, cins[1], ("cin", 1), halo_fn)
            conv_post(8 + 2 * h, kTb[:, 0, :], "convdst", 1.0 / 16.0, cins[0], ("cin", 0), cacc)
            vproj(0)
            conv_post(9 + 2 * h, kTb[:, 1, :], "convdst", 1.0 / 16.0, cins[1], ("cin", 1), cacc)
            vproj(1)
            transposes_to_tok(kTb, lambda n: ktok[:, n, :], "ktok")
            transposes_to_tok(vTb, lambda n: vaug[:, n, 0:256], "vaug", stok=vtok)

        mX = A.mark()
        T_all = A.alloc([3, 4, 528], F32)
        P.op("pool", lambda e: e.memset(T_all, 0.0), writes=["T_all"])
        for o in range(3):
            P.mark("X%d_norm" % o)
            tiles = []
            for tl in range(16):
                tiles.append((xo_d[o, tl * 128:(tl + 1) * 128, :],
                              (lambda hb, tl=tl: hnT[:, hb * 8:(hb + 1) * 8, tl * 128:(tl + 1) * 128])))
            tiles.append((xoh_d[o], (lambda hb: hnTh[:, hb * 8:(hb + 1) * 8, 0:128])))
            norm_phase(tiles)
            P.mark("X%d_gates" % o)
            m1 = gates_pass(40)
            P.mark("X%d_heads" % o)
            P.op("dve", lambda e, o=o: e.tensor_copy(out=GO[:, o, :], in_=GSEG), reads=["GSEG"], writes=["GO"])
            wbl = A.alloc([16, 4], F32)
            cins = [A.alloc([2052], F32) for _ in range(2)]
            cacc = A.alloc([2048], F32)
            kTb = A.alloc([2, 2048], BF16)
            ktok = A.alloc([16, 256], BF16)
            vTb = A.alloc([2, 2048], BF16)
            vaug = A.alloc([16, 264], BF16)
            khs = [A.alloc([256], BF16) for _ in range(8)]
            P.op("dve", lambda e, o=o, wbl=wbl: e.tensor_scalar(out=wbl, in0=WSEG[:, :, 0:4], scalar1=cf[:, o:o + 1], scalar2=None, op0=ALU.mult),
                 reads=["WSEG", "cf"], writes=["wbl"])
            P.op("dve", lambda e, o=o, wbl=wbl: e.scalar_tensor_tensor(out=wbl, in0=WSEG[:, :, 4:8], scalar=cf[:, 3 + o:4 + o], in1=wbl,
                                                                        op0=ALU.mult, op1=ALU.add),
                 reads=["WSEG", "cf", "wbl"], writes=["wbl"])
            P.op("pool", lambda e, vaug=vaug: e.memset(vaug, 0.0), writes=["vaug"])
            P.op("pool", lambda e, vaug=vaug: e.memset(vaug[:, :, 256:257], 1.0), writes=["vaug"])
            for h in range(4):
                kv_operands(h, kTb, ktok, vTb, vaug, cins, cacc, lambda kc: hnTh[:, kc, 0:4])
                def kscale(n):
                    kb = khs[n % 8]
                    P.op("act", lambda e, n=n, h=h, kb=kb, ktok=ktok, wbl=wbl: e.activation(out=kb, in_=ktok[:, n, :], func=AF.Copy,
                                                                                          scale=wbl[:, n, h:h + 1]),
                         reads=["ktok", "wbl"], writes=[("khs", n % 8)])
                for n in range(8):
                    kscale(n)
                for n in range(16):
                    kb = khs[n % 8]
                    if n >= 1 and n + 7 < 16:
                        kscale(n + 7)
                    for ec in range(2):
                        P.op("pe", lambda e, n=n, ec=ec, kb=kb, vaug=vaug: e.matmul(out=PS(ec)[:, 0:257], lhsT=kb[:, ec * 128:(ec + 1) * 128],
                                                                                    rhs=vaug[:, n, 0:257], start=(n == 0), stop=(n == 15)),
                             reads=[("khs", n % 8), "vaug"], writes=[pst(ec)])
                for ec in range(2):
                    P.op("dve", lambda e, ec=ec, o=o, h=h: e.tensor_copy(out=T_all[:, o, h, ec * 264:ec * 264 + 257], in_=PS(ec)[:, 0:257]),
                         reads=[pst(ec)], writes=["T_all"])
            P.barrier()
            A.release(m1)
        P.mark("X_combine")
        COEF = A.alloc([3, 8], F32)
        cacc2 = A.alloc([3, 4], F32)
        cstg = A.alloc([8, 528], F32)
        for d in range(2):
            for o in range(3):
                for u in range(3):
                    col = 6 + d * 9 + o * 3 + u
                    if u == 0:
                        P.op("dve", lambda e, o=o, u=u, col=col, d=d: e.tensor_scalar(out=cacc2[:, o, :], in0=GO[:, u, d * 4:(d + 1) * 4],
                                                                                     scalar1=cf[:, col:col + 1], scalar2=None, op0=ALU.mult),
                             reads=["GO", "cf"], writes=["cacc2"])
                    else:
                        P.op("dve", lambda e, o=o, u=u, col=col, d=d: e.scalar_tensor_tensor(out=cacc2[:, o, :], in0=GO[:, u, d * 4:(d + 1) * 4],
                                                                                            scalar=cf[:, col:col + 1], in1=cacc2[:, o, :],
                                                                                            op0=ALU.mult, op1=ALU.add),
                             reads=["GO", "cf", "cacc2"], writes=["cacc2"])
            P.op("act", lambda e: e.activation(out=cacc2, in_=cacc2, func=AF.Exp), reads=["cacc2"], writes=["cacc2"])
            for o in range(3):
                col = d * 3 + o
                P.op("dve", lambda e, o=o, col=col, d=d: e.tensor_scalar(out=COEF[:, o, d * 4:(d + 1) * 4], in0=cacc2[:, o, :],
                                                                        scalar1=cf[:, col:col + 1], scalar2=None, op0=ALU.mult),
                     reads=["cacc2", "cf"], writes=["COEF"])
        for d in range(2):
            for h in range(4):
                c = d * 4 + h
                for o in range(3):
                    if o == 0:
                        P.op("dve", lambda e, c=c, o=o, h=h: e.tensor_scalar(out=cstg[:, c, :], in0=T_all[:, o, h, :], scalar1=COEF[:, o, c:c + 1],
                                                                            scalar2=None, op0=ALU.mult),
                             reads=["T_all", "COEF"], writes=["cstg"])
                    else:
                        P.op("dve", lambda e, c=c, o=o, h=h: e.scalar_tensor_tensor(out=cstg[:, c, :], in0=T_all[:, o, h, :], scalar=COEF[:, o, c:c + 1],
                                                                                   in1=cstg[:, c, :], op0=ALU.mult, op1=ALU.add),
                             reads=["T_all", "COEF", "cstg"], writes=["cstg"])
        P.dma("sp", cin_s.rearrange("c p f -> p c f"), cstg, reads=["cstg"], writes=["cin_s"])
        P.barrier()
        A.release(mX)
        if stop_after == "X":
            P.emit()
            return nc

        tiles = []
        for tl in range(20):
            if tl < 16:
                tiles.append((xe[tl * 128:(tl + 1) * 128, :], (lambda hb, tl=tl: hnT[:, hb * 8:(hb + 1) * 8, tl * 128:(tl + 1) * 128])))
            else:
                tiles.append((xe[tl * 128:(tl + 1) * 128, :], (lambda hb, tl=tl: hnTh[:, hb * 8:(hb + 1) * 8, (tl - 16) * 128:(tl - 15) * 128])))
        P.mark("A_norm")
        norm_phase(tiles)
        P.op("dve", lambda e: e.tensor_copy(out=hhalo, in_=hnTh[:, :, 254:258]), reads=["hnT"], writes=["hhalo"])
        P.mark("A_gates")
        mN = gates_pass(0)
        P.mark("N")

        QTs = [A.alloc([2048], BF16) for _ in range(2)]
        KTs = [A.alloc([2560], BF16) for _ in range(2)]
        VTs = [A.alloc([2560], BF16) for _ in range(2)]
        GTs = [A.alloc([2048], BF16) for _ in range(2)]
        Vtoks = [A.alloc([20, 128], BF16) for _ in range(2)]
        BGs = [A.alloc([640], F32) for _ in range(2)]
        BFs = [A.alloc([896], F32) for _ in range(2)]
        attT = A.alloc([2048], BF16)
        scs = [A.alloc([768], F32) for _ in range(2)]
        PTs = [A.alloc([768], BF16) for _ in range(2)]
        rden = [A.alloc([128], F32) for _ in range(2)]
        atmp = [A.alloc([128], F32) for _ in range(2)]

        def proj_gen(h):
            p = h % 2
            QT, KT, VT, GTb, Vtok = QTs[p], KTs[p], VTs[p], GTs[p], Vtoks[p]
            P.dma("sp", BGs[p], bg_d[h], writes=[("BG", p)])
            P.dma("sp", BFs[p], bf_d[h], writes=[("BF", p)])
            wi = load_w(h)
            for tt in range(4):
                bank = next_bank(0, 2)
                proj_mm(wi, rhs_main(tt), 512, bank)
                yield
                P.op("act", lambda e, bank=bank, tt=tt, QT=QT: e.activation(out=QT[:, tt * 512:(tt + 1) * 512], in_=PS(bank)[:, 0:512],
                                                                            func=AF.Copy, scale=128.0 ** -0.5),
                     reads=[pst(bank)], writes=[("QT", p)])
                yield
            for g, dst, tok in ((8 + h, KT, ("KT", p)), (16 + h, VT, ("VT", p))):
                wi = load_w(g)
                bank = next_bank(0, 2)
                proj_mm(wi, lambda kc: hnTh[:, kc, 0:512], 512, bank)
                yield
                P.op("act", lambda e, bank=bank, dst=dst: e.copy(out=dst[:, 0:256], in_=PS(bank)[:, 0:256]), reads=[pst(bank)], writes=[tok])
                P.op("dve", lambda e, bank=bank, dst=dst: e.tensor_copy(out=dst[:, 2304:2560], in_=PS(bank)[:, 256:512]), reads=[pst(bank)], writes=[tok])
                yield
                for tt in range(4):
                    bank = next_bank(0, 2)
                    proj_mm(wi, rhs_main(tt), 512, bank)
                    yield
                    if tt % 2 == 0:
                        P.op("act", lambda e, bank=bank, tt=tt, dst=dst: e.copy(out=dst[:, 256 + tt * 512:256 + (tt + 1) * 512], in_=PS(bank)[:, 0:512]),
                             reads=[pst(bank)], writes=[tok])
                    else:
                        P.op("dve", lambda e, bank=bank, tt=tt, dst=dst: e.tensor_copy(out=dst[:, 256 + tt * 512:256 + (tt + 1) * 512], in_=PS(bank)[:, 0:512]),
                             reads=[pst(bank)], writes=[tok])
                    yield
            for t4 in range(5):
                bank = next_bank(0, 2)
                for tq in range(4):
                    t = t4 * 4 + tq
                    P.op("pe", lambda e, t=t, tq=tq, bank=bank, VT=VT: e.transpose(out=PSB(bank)[:, tq * 128:(tq + 1) * 128],
                                                                                   in_=VT[:, t * 128:(t + 1) * 128], identity=identb),
                         reads=[("VT", p), "identb"], writes=[pst(bank)])
                yield
                P.op("dve", lambda e, t4=t4, bank=bank, Vtok=Vtok: e.tensor_copy(out=Vtok[:, t4 * 4:(t4 + 1) * 4, :],
                                                                                in_=PSB(bank)[:, 0:512].rearrange("p (a b) -> p a b", a=4)),
                     reads=[pst(bank)], writes=[("Vtok", p)])
                yield

            wi = load_w(24 + h)
            for tt in range(4):
                bank = next_bank(0, 2)
                proj_mm(wi, rhs_main(tt), 512, bank)
                yield
                P.op("act", lambda e, bank=bank, tt=tt, GTb=GTb: e.activation(out=GTb[:, tt * 512:(tt + 1) * 512], in_=PS(bank)[:, 0:512], func=AF.Silu),
                     reads=[pst(bank)], writes=[("GT", p)])
                yield

        def tiles_of(b):
            if b == 0:
                return [0, 1, 2, 3, 4, 5]
            if b == 15:
                return [14, 15, 16, 17, 18, 19]
            return [b + j for j in range(5)]

        def scores(h, b):
            p = h % 2
            i2 = b % 2
            sbank = [2 + 2 * i2, 3 + 2 * i2]
            for i, t in enumerate(tiles_of(b)):
                bk = sbank[i // 4]
                P.op("pe", lambda e, t=t, i=i, bk=bk, b=b, p=p: e.matmul(out=PS(bk)[:, (i % 4) * 128:(i % 4 + 1) * 128],
                                                                         lhsT=KTs[p][:, t * 128:(t + 1) * 128],
                                                                         rhs=QTs[p][:, b * 128:(b + 1) * 128], start=True, stop=True),
                     reads=[("KT", p), ("QT", p)], writes=[pst(bk)])

        def softmax(h, b):
            p = h % 2
            BG, BFt = BGs[p], BFs[p]
            i2 = b % 2
            sc, PT = scs[i2], PTs[i2]
            sbank = [2 + 2 * i2, 3 + 2 * i2]
            tiles = tiles_of(b)
            nt = len(tiles)
            if b not in (0, 1, 14, 15):
                P.op("dve", lambda e, sc=sc, sbank=sbank, BG=BG: e.tensor_tensor(out=sc[:, 0:512], in0=PS(sbank[0])[:, 0:512], in1=BG[:, 0:512], op=ALU.add),
                     reads=[pst(sbank[0]), ("BG", p)], writes=[("sc", i2)])
                P.op("dve", lambda e, sc=sc, sbank=sbank, BG=BG: e.tensor_tensor(out=sc[:, 512:640], in0=PS(sbank[1])[:, 0:128], in1=BG[:, 512:640], op=ALU.add),
                     reads=[pst(sbank[1]), ("BG", p)], writes=[("sc", i2)])
            else:
                for i, t in enumerate(tiles):
                    bk = sbank[i // 4]
                    jr = t - b
                    if (b, t) in SPECIAL:
                        tab = BFt[:, (jr + 1) * 128:(jr + 2) * 128]
                    else:
                        tab = BG[:, jr * 128:(jr + 1) * 128]
                    P.op("dve", lambda e, sc=sc, i=i, bk=bk, tab=tab: e.tensor_tensor(out=sc[:, i * 128:(i + 1) * 128],
                                                                                     in0=PS(bk)[:, (i % 4) * 128:(i % 4 + 1) * 128],
                                                                                     in1=tab, op=ALU.add),
                         reads=[pst(bk), ("BG", p), ("BF", p)], writes=[("sc", i2)])
                    if (b, t) in SPECIAL:
                        mi = SPECIAL.index((b, t))
                        P.op("dve", lambda e, sc=sc, i=i, mi=mi: e.tensor_tensor(out=sc[:, i * 128:(i + 1) * 128], in0=sc[:, i * 128:(i + 1) * 128],
                                                                                in1=msk[:, mi * 128:(mi + 1) * 128], op=ALU.add),
                             reads=[("sc", i2), "msk"], writes=[("sc", i2)])
            P.op("act", lambda e, sc=sc, PT=PT, nt=nt: e.activation(out=PT[:, 0:nt * 128], in_=sc[:, 0:nt * 128], func=AF.Exp),
                 reads=[("sc", i2)], writes=[("PT", i2)])

        def pv(h, b):
            p = h % 2
            i2 = b % 2
            PT = PTs[i2]
            tiles = tiles_of(b)
            nt = len(tiles)
            for i, t in enumerate(tiles):
                P.op("pe", lambda e, t=t, i=i, PT=PT, nt=nt, p=p: e.matmul(out=PS(6)[:, 0:128], lhsT=Vtoks[p][:, t, :], rhs=PT[:, i * 128:(i + 1) * 128],
                                                                          start=(i == 0), stop=(i == nt - 1)),
                     reads=[("Vtok", p), ("PT", i2)], writes=[pst(6)])
            for i, t in enumerate(tiles):
                P.op("pe", lambda e, i=i, PT=PT, nt=nt: e.matmul(out=PS(7)[:, 0:128], lhsT=onesb, rhs=PT[:, i * 128:(i + 1) * 128],
                                                                start=(i == 0), stop=(i == nt - 1)),
                     reads=["onesb", ("PT", i2)], writes=[pst(7)])
            rd, at = rden[i2], atmp[i2]
            P.op("act", lambda e, rd=rd: e.activation(out=rd, in_=PS(7)[:, 0:128], func=AF.Ln), reads=[pst(7)], writes=[("rd", i2)])
            P.op("act", lambda e, rd=rd: e.activation(out=rd, in_=rd, func=AF.Exp, scale=-1.0), reads=[("rd", i2)], writes=[("rd", i2)])
            P.op("dve", lambda e, rd=rd, at=at: e.tensor_tensor(out=at, in0=PS(6)[:, 0:128], in1=rd, op=ALU.mult),
                 reads=[pst(6), ("rd", i2)], writes=[("at", i2)])
            P.op("dve", lambda e, at=at, b=b, p=p: e.tensor_tensor(out=attT[:, b * 128:(b + 1) * 128], in0=at, in1=GTs[p][:, b * 128:(b + 1) * 128], op=ALU.mult),
                 reads=[("at", i2), ("GT", p)], writes=["attT"])

        for _ in proj_gen(0):
            pass
        for h in range(8):
            gen = proj_gen(h + 1) if h < 7 else iter(())
            scores(h, 0)
            scores(h, 1)
            softmax(h, 0)
            for b in range(16):
                if b + 2 < 16:
                    scores(h, b + 2)
                next(gen, None)
                if b + 1 < 16:
                    softmax(h, b + 1)
                pv(h, b)
                next(gen, None)
            for _ in gen:
                pass
            for hf in range(2):
                P.dma("sp", mixT[hf * 8:(hf + 1) * 8, :, h, :].rearrange("a p t -> p a t"),
                      attT[:, hf * 1024:(hf + 1) * 1024].rearrange("p (a t) -> p a t", a=8), reads=["attT"], writes=[("mixT", h)])
        prefetch_w(40)
        P.barrier()
        A.release(pre_halo_mark)
        if stop_after == "N":
            P.emit()
            return nc

        P.mark("M")
        mM = A.mark()
        qT = A.alloc([2, 2048], BF16)
        kT2 = A.alloc([2, 2048], BF16)
        ktok2 = A.alloc([16, 256], BF16)
        vaug2 = A.alloc([16, 264], BF16)
        goz = A.alloc([2, 2048], BF16)
        Hs = A.alloc([16, 256], F32)
        C32 = [A.alloc([2, 264], F32) for _ in range(2)]
        Cbf = [[A.alloc([2, 264], BF16) for _ in range(2)] for _ in range(2)]
        PTm = [[A.alloc([128], BF16) for _ in range(2)] for _ in range(2)]
        khm = [A.alloc([256], BF16) for _ in range(2)]
        rl = [A.alloc([2], F32) for _ in range(2)]
        tmpH = [A.alloc([256], F32) for _ in range(2)]
        hsn = A.alloc([256], BF16)
        hss = A.alloc([2], F32)
        memT = kT2
        cins2 = [A.alloc([2052], F32) for _ in range(2)]
        cacc2b = A.alloc([2048], F32)
        zbuf = A.alloc([2048], F32)
        stile = [A.alloc([512], F32) for _ in range(2)]
        vT2 = A.alloc([2, 2048], BF16)
        own_halo = lambda kc: hhalo[:, kc, 0:4]

        def gate_z(g_z):
            wi = load_w(g_z)
            for tt in range(4):
                bank = next_bank(0, 2)
                proj_mm(wi, rhs_main(tt), 512, bank)
                P.op("act", lambda e, bank=bank, tt=tt: e.activation(out=zbuf[:, tt * 512:(tt + 1) * 512], in_=PS(bank)[:, 0:512], func=AF.Silu),
                     reads=[pst(bank)], writes=["zbuf"])

        def gate_o(g_o, ec):
            wi = load_w(g_o)
            for tt in range(4):
                bank = next_bank(0, 2)
                st = stile[tt % 2]
                proj_mm(wi, rhs_main(tt), 512, bank)
                P.op("act", lambda e, bank=bank, st=st: e.activation(out=st, in_=PS(bank)[:, 0:512], func=AF.Sigmoid),
                     reads=[pst(bank)], writes=[("stile", tt % 2)])
                P.op("dve", lambda e, tt=tt, st=st: e.tensor_tensor(out=goz[:, ec, tt * 512:(tt + 1) * 512], in0=st, in1=zbuf[:, tt * 512:(tt + 1) * 512],
                                                                   op=ALU.mult),
                     reads=[("stile", tt % 2), "zbuf"], writes=["goz"])

        P.op("pool", lambda e: e.memset(vaug2, 0.0), writes=["vaug"])
        P.op("pool", lambda e: e.memset(vaug2[:, :, 256:257], 1.0), writes=["vaug"])
        for h in range(4):
            P.mark("M%d_proj" % h)
            kv_operands(h, kT2, ktok2, vT2, vaug2, cins2, cacc2b, own_halo)
            conv_proj(32 + 2 * h, cins2[0], ("cin", 0), own_halo)
            conv_proj(33 + 2 * h, cins2[1], ("cin", 1), own_halo)
            gate_z(64 + 2 * h)
            conv_post(2 * h, qT[:, 0, :], "convdst", None, cins2[0], ("cin", 0), cacc2b)
            gate_o(56 + 2 * h, 0)
            gate_z(65 + 2 * h)
            conv_post(2 * h + 1, qT[:, 1, :], "convdst", None, cins2[1], ("cin", 1), cacc2b)
            gate_o(57 + 2 * h, 1)
            for d in range(2):
                c = d * 4 + h
                P.dma("sp", C32[d].rearrange("p a b -> p (a b)"), cin_s[c], reads=["cin_s"], writes=[("C32", d)])
                P.op("act", lambda e, d=d: e.copy(out=Cbf[d][0], in_=C32[d]), reads=[("C32", d)], writes=[("Cbf", d, 0)])
            P.mark("M%d_scan" % h)
            def chunk_of(j, d):
                return j if d == 0 else 15 - j

            def st_scores(j, d):
                n, c, bS = chunk_of(j, d), d * 4 + h, 4 * d
                for ec in range(2):
                    P.op("pe", lambda e, n=n, ec=ec, bS=bS: e.matmul(out=PS(bS)[:, 0:128], lhsT=kT2[:, ec, n * 128:(n + 1) * 128],
                                                                     rhs=qT[:, ec, n * 128:(n + 1) * 128], start=(ec == 0), stop=(ec == 1)),
                         reads=["convdst"], writes=[pst(bS)])
                P.op("dve", lambda e, n=n, c=c, d=d, bS=bS, j=j: e.scalar_tensor_tensor(out=PTm[d][j % 2], in0=PS(bS)[:, 0:128], scalar=WP[:, n, c:c + 1],
                                                                                       in1=tri[:, d * 128:(d + 1) * 128], op0=ALU.mult, op1=ALU.mult),
                     reads=[pst(bS), "WP", "tri"], writes=[("PTm", d, j % 2)])

            def st_state(j, d):
                n, c = chunk_of(j, d), d * 4 + h
                bK0, bK1 = 4 * d + 2, 4 * d + 3
                nxt = (j + 1) % 2
                for ec, bk in ((0, bK0), (1, bK1)):
                    P.op("pe", lambda e, n=n, ec=ec, d=d, bk=bk: e.matmul(out=PS(bk)[:, 0:257], lhsT=khm[d][:, ec * 128:(ec + 1) * 128],
                                                                          rhs=vaug2[:, n, 0:257], start=True, stop=True),
                         reads=[("khm", d), "vaug"], writes=[pst(bk)])
                for ec, bk in ((0, bK0), (1, bK1)):
                    P.op("dve", lambda e, n=n, c=c, d=d, ec=ec, bk=bk: e.scalar_tensor_tensor(out=C32[d][:, ec, 0:257], in0=C32[d][:, ec, 0:257],
                                                                                             scalar=EG[:, n, c:c + 1], in1=PS(bk)[:, 0:257],
                                                                                             op0=ALU.mult, op1=ALU.add),
                         reads=[("C32", d), "EG", pst(bk)], writes=[("C32", d)])
                P.op("act", lambda e, d=d, nxt=nxt: e.copy(out=Cbf[d][nxt], in_=C32[d]), reads=[("C32", d)], writes=[("Cbf", d, nxt)])
                if j + 1 < 16:
                    n1 = chunk_of(j + 1, d)
                    P.op("act", lambda e, n1=n1, c=c, d=d: e.activation(out=khm[d], in_=ktok2[:, n1, :], func=AF.Copy, scale=KH[:, n1, c:c + 1]),
                         reads=["ktok", "KH"], writes=[("khm", d)])

            def st_out(j, d):
                n, c, bO, cur = chunk_of(j, d), d * 4 + h, 4 * d + 1, j % 2
                for ec in range(2):
                    P.op("pe", lambda e, n=n, ec=ec, d=d, bO=bO, cur=cur: e.matmul(out=PS(bO)[:, 0:257], lhsT=qT[:, ec, n * 128:(n + 1) * 128],
                                                                                   rhs=Cbf[d][cur][:, ec, 0:257], start=(ec == 0), stop=False),
                         reads=["convdst", ("Cbf", d, cur)], writes=[pst(bO)])
                P.op("pe", lambda e, n=n, d=d, bO=bO, cur=cur: e.matmul(out=PS(bO)[:, 0:257], lhsT=PTm[d][cur], rhs=vaug2[:, n, 0:257], start=False, stop=True),
                     reads=[("PTm", d, cur), "vaug"], writes=[pst(bO)])
                P.op("act", lambda e, d=d, bO=bO: e.activation(out=rl[d][:, 0:1], in_=PS(bO)[:, 256:257], func=AF.Abs),
                     reads=[pst(bO)], writes=[("rl", d)])
                P.op("pool", lambda e, n=n, c=c, d=d: e.tensor_scalar(out=rl[d][:, 0:1], in0=rl[d][:, 0:1], scalar1=ENB[:, n, c:c + 1],
                                                                     scalar2=None, op0=ALU.max),
                     reads=[("rl", d), "ENB"], writes=[("rl", d)])
                P.op("dve", lambda e, d=d: e.reciprocal(out=rl[d][:, 1:2], in_=rl[d][:, 0:1]), reads=[("rl", d)], writes=[("rl1", d)])
                if j < 8:
                    P.op("act", lambda e, n=n, d=d, bO=bO: e.activation(out=Hs[:, n, :], in_=PS(bO)[:, 0:256], func=AF.Copy, scale=rl[d][:, 1:2]),
                         reads=[pst(bO), ("rl1", d)], writes=[("Hs", n)])
                else:
                    P.op("act", lambda e, d=d, bO=bO: e.activation(out=tmpH[d], in_=PS(bO)[:, 0:256], func=AF.Copy, scale=rl[d][:, 1:2]),
                         reads=[pst(bO), ("rl1", d)], writes=[("tmpH", d)])
                    P.op("pool", lambda e, n=n, d=d: e.tensor_tensor(out=Hs[:, n, :], in0=Hs[:, n, :], in1=tmpH[d], op=ALU.add),
                         reads=[("tmpH", d), ("Hs", n)], writes=[("Hs", n)])

            for d in range(2):
                c = d * 4 + h
                n0 = chunk_of(0, d)
                P.op("act", lambda e, n0=n0, c=c, d=d: e.activation(out=khm[d], in_=ktok2[:, n0, :], func=AF.Copy, scale=KH[:, n0, c:c + 1]),
                     reads=["ktok", "KH"], writes=[("khm", d)])
                st_scores(0, d)
            for j in range(16):
                for d in range(2):
                    if j + 1 < 16:
                        st_scores(j + 1, d)
                    st_state(j, d)
                    st_out(j, d)
            P.mark("M%d_post" % h)
            for n in range(16):
                P.op("act", lambda e, n=n: e.activation(out=hsn, in_=Hs[:, n, :], func=AF.Square, accum_out=hss[:, 0:1]),
                     reads=[("Hs", n)], writes=["hsn", "hss"])
                P.op("dve", lambda e: e.tensor_scalar(out=hss[:, 1:2], in0=hss[:, 0:1], scalar1=1.0 / 256, scalar2=EPS, op0=ALU.mult, op1=ALU.add),
                     reads=["hss"], writes=["hss1"])
                P.op("act", lambda e: e.activation(out=hss[:, 1:2], in_=hss[:, 1:2], func=AF.Sqrt), reads=["hss1"], writes=["hss1"])
                P.op("dve", lambda e: e.reciprocal(out=hss[:, 1:2], in_=hss[:, 1:2]), reads=["hss1"], writes=["hss1"])
                P.op("dve", lambda e, n=n: e.tensor_scalar(out=hsn, in0=Hs[:, n, :], scalar1=hss[:, 1:2], scalar2=None, op0=ALU.mult),
                     reads=[("Hs", n), "hss1", "hsn"], writes=["hsn"])
                bank = 2 + n % 2
                for ec in range(2):
                    P.op("pe", lambda e, ec=ec, bank=bank: e.transpose(out=PSB(bank)[:, ec * 128:(ec + 1) * 128], in_=hsn[:, ec * 128:(ec + 1) * 128],
                                                                       identity=identb), reads=["hsn", "identb"], writes=[pst(bank)])
                for ec in range(2):
                    P.op("dve", lambda e, ec=ec, n=n, bank=bank, h=h: e.scalar_tensor_tensor(
                        out=memT[:, ec, n * 128:(n + 1) * 128], in0=PSB(bank)[:, ec * 128:(ec + 1) * 128],
                        scalar=mnw_s[:, h * 2 + ec:h * 2 + ec + 1], in1=goz[:, ec, n * 128:(n + 1) * 128], op0=ALU.mult, op1=ALU.mult),
                        reads=[pst(bank), "mnw", "goz"], writes=["convdst"])
            for ec in range(2):
                for hf in range(2):
                    P.dma("pool", mixT[hf * 8:(hf + 1) * 8, :, 8 + 2 * h + ec, :].rearrange("a p t -> p a t"),
                          memT[:, ec, hf * 1024:(hf + 1) * 1024].rearrange("p (a t) -> p a t", a=8),
                          reads=["convdst"], writes=[("mixT", 8 + 2 * h + ec)])
        P.barrier()
        A.release(pre_hn_mark)
        if stop_after == "M":
            P.emit()
            return nc

        P.mark("O")
        WO = A.alloc([16, 2048], BF16)
        fw = A.alloc([D], F32)
        mts = [A.alloc([16, 128], BF16) for _ in range(3)]
        xos = [A.alloc([D], F32) for _ in range(3)]
        ys = [A.alloc([D], F32) for _ in range(2)]
        junk2 = A.alloc([D], BF16)
        so = [A.alloc([8], F32) for _ in range(2)]
        w_out3 = w_out.rearrange("p (a b) -> p a b", a=16)
        for q4 in range(4):
            P.dma("pool", WO[:, :, q4 * 512:(q4 + 1) * 512], w_out3[:, :, q4 * 512:(q4 + 1) * 512], writes=[("WO", q4)])
        P.dma("sp", fw, fnw, writes=["fw"])
        outs = []
        for tl in range(16):
            i = tl % 2
            i3 = tl % 3
            mt, xo, yy, s8 = mts[i3], xos[i3], ys[i], so[i]
            P.dma("sp", mt.rearrange("p a b -> p (a b)"), mixT[tl].rearrange("p a b -> p (a b)"),
                  reads=[("mixT", c) for c in range(16)], writes=[("mt", i3)])
            P.dma("sp", xo, xe[tl * 128:(tl + 1) * 128, :], writes=[("xo", i3)])
            for dq in range(4):
                bank = (tl % 2) * 4 + dq
                for mc in range(16):
                    P.op("pe", lambda e, mc=mc, dq=dq, bank=bank, mt=mt: e.matmul(out=PS(bank)[:, 0:512], lhsT=mt[:, mc, :],
                                                                                  rhs=WO[:, mc, dq * 512:(dq + 1) * 512],
                                                                                  start=(mc == 0), stop=(mc == 15)),
                         reads=[("mt", i3), ("WO", dq)], writes=[pst(bank)])
                P.op("dve", lambda e, dq=dq, bank=bank, xo=xo, yy=yy: e.tensor_tensor(out=yy[:, dq * 512:(dq + 1) * 512], in0=PS(bank)[:, 0:512],
                                                                                      in1=xo[:, dq * 512:(dq + 1) * 512], op=ALU.add),
                     reads=[pst(bank), ("xo", i3)], writes=[("yy", i)])
            P.op("act", lambda e, yy=yy, s8=s8: e.activation(out=junk2, in_=yy, func=AF.Square, accum_out=s8[:, 0:1]),
                 reads=[("yy", i)], writes=["junk2", ("s8", i)])
            P.op("dve", lambda e, s8=s8: e.tensor_scalar(out=s8[:, 1:2], in0=s8[:, 0:1], scalar1=1.0 / D, scalar2=EPS, op0=ALU.mult, op1=ALU.add),
                 reads=[("s8", i)], writes=[("s81", i)])
            P.op("act", lambda e, s8=s8: e.activation(out=s8[:, 1:2], in_=s8[:, 1:2], func=AF.Sqrt), reads=[("s81", i)], writes=[("s81", i)])
            P.op("dve", lambda e, s8=s8: e.reciprocal(out=s8[:, 1:2], in_=s8[:, 1:2]), reads=[("s81", i)], writes=[("s81", i)])
            P.op("dve", lambda e, yy=yy, s8=s8: e.scalar_tensor_tensor(out=yy, in0=yy, scalar=s8[:, 1:2], in1=fw, op0=ALU.mult, op1=ALU.mult),
                 reads=[("yy", i), ("s81", i), "fw"], writes=[("yy", i)])
            P.dma("pool", y_out[tl * 128:(tl + 1) * 128, :], yy, reads=[("yy", i)], writes=[("yout", tl)])
            outs.append(("yout", tl))
        P.wait_all("sp", outs)
        P.emit()
    build_program.last_prog = P
    return nc


def _na_tables(rpb):
    cs = np.clip(np.arange(64) - 8, 0, 48)
    kc = np.arange(64)[:, None]
    c = np.arange(64)[None, :]
    colok = (kc >= cs[None, :]) & (kc <= cs[None, :] + 15)
    coff = np.clip(kc - c + 15, 0, 30)
    bg = np.full((8, 128, 640), NEG, np.float32)
    bf = np.full((8, 128, 896), NEG, np.float32)
    for jr in range(-1, 6):
        for jj in range(2):
            for ii in range(2):
                dk = 2 * jr + jj - ii - 4
                if abs(dk) > 7:
                    continue
                blk = np.where(colok[None], rpb[:, dk + 7][:, coff], np.float32(NEG)).astype(np.float32)
                bf[:, jj * 64:(jj + 1) * 64, (jr + 1) * 128 + ii * 64:(jr + 1) * 128 + (ii + 1) * 64] = blk
                if 0 <= jr <= 4 and -4 <= dk <= 3:
                    bg[:, jj * 64:(jj + 1) * 64, jr * 128 + ii * 64:jr * 128 + (ii + 1) * 64] = blk
    return bg, bf


def _core_masks(r0, rows):
    m = np.full((128, 12 * 128 + 4), NEG, np.float32)
    for idx, (b, t) in enumerate(SPECIAL):
        for jj in range(2):
            for ii in range(2):
                r = r0 + 2 * b + ii
                k = r0 + 2 * t + jj - 4
                rs = min(max(r - 4, 0), rows - 8)
                ok = (0 <= k < rows) and (rs <= k <= rs + 7)
                if ok:
                    m[jj * 64:(jj + 1) * 64, idx * 128 + ii * 64: idx * 128 + (ii + 1) * 64] = 0.0
    return m


def _prepare(x_prompt, x_sample, ln_w, w_in, na_rpb, ml_conv_w, ml_conv_b, ml_gate_b, ml_norm_w, w_out, final_norm_w):
    f = np.float32
    x_prompt = np.asarray(x_prompt, f)
    x_sample = np.asarray(x_sample, f)
    w_in0 = np.asarray(w_in, f)[0]
    wpad = np.zeros((D, NGRP * 128), f)
    wpad[:, :w_in0.shape[1]] = w_in0
    w_in_l = np.ascontiguousarray(wpad.reshape(16, 128, NGRP, 128).transpose(2, 1, 0, 3).reshape(NGRP, 128, 2048))
    w_out_l = np.ascontiguousarray(np.asarray(w_out, f)[0].reshape(16, 128, 2048).transpose(1, 0, 2).reshape(128, 16 * 2048))
    lnw_r = np.ascontiguousarray(np.tile(np.asarray(ln_w, f)[0][None], (128, 1)))
    fnw_r = np.ascontiguousarray(np.tile(np.asarray(final_norm_w, f)[None], (128, 1)))
    cwl = np.ascontiguousarray(np.asarray(ml_conv_w, f)[0].reshape(5, 16, 128).transpose(2, 1, 0).reshape(128, 80))
    cbl = np.ascontiguousarray(np.asarray(ml_conv_b, f)[0].reshape(16, 128).T)
    gbl = np.ascontiguousarray(np.tile(np.asarray(ml_gate_b, f)[0].reshape(1, 16), (128, 16)))
    mnwl = np.ascontiguousarray(np.asarray(ml_norm_w, f)[0].reshape(8, 128).T)
    ident = np.eye(128, dtype=f)
    tri = np.concatenate([np.triu(np.ones((128, 128), f)), np.tril(np.ones((128, 128), f))], axis=1)
    bg, bf = _na_tables(np.asarray(na_rpb, f)[0])

    in_maps = []
    for core in range(8):
        xe = np.zeros((2560, D), f)
        xo = np.zeros((3, 2048, D), f)
        xoh = np.zeros((3, 128, D), f)
        cfv = np.zeros((64,), f)
        if core < 4:
            seq = x_prompt[0]
            t0 = core * 2048
            r0, rows = core * 32, 128
            others = [i for i in range(4) if i != core]
            for o, i in enumerate(others):
                xo[o] = seq[i * 2048:(i + 1) * 2048]
                if i > 0:
                    xoh[o, 0:2] = seq[i * 2048 - 2:i * 2048]
                if i < 3:
                    xoh[o, 2:4] = seq[(i + 1) * 2048:(i + 1) * 2048 + 2]
                cfv[o] = 1.0 if i < core else 0.0
                cfv[3 + o] = 1.0 if i > core else 0.0
                for u, iu in enumerate(others):
                    cfv[6 + o * 3 + u] = 1.0 if i < iu < core else 0.0
                    cfv[15 + o * 3 + u] = 1.0 if core < iu < i else 0.0
        else:
            seq = x_sample[core - 4]
            t0 = 0
            r0, rows = 0, 32
        T = seq.shape[0]
        xe[0:2048] = seq[t0:t0 + 2048]
        if t0 - 256 >= 0:
            xe[2048:2304] = seq[t0 - 256:t0]
        if t0 + 2048 + 256 <= T:
            xe[2304:2560] = seq[t0 + 2048:t0 + 2304]
        in_maps.append({
            "xe": xe, "xo": xo, "xoh": xoh, "lnw": lnw_r, "fnw": fnw_r, "w_in": w_in_l, "w_out": w_out_l, "convw": cwl, "convb": cbl,
            "gateb": gbl, "mnw": mnwl, "ident": ident, "tri": tri, "bg": bg, "bf": bf,
            "msk": _core_masks(r0, rows), "cf": np.ascontiguousarray(np.tile(cfv[None], (128, 1))),
        })
    return in_maps


def kernel(x_prompt, x_sample, ln_w, w_in, na_rpb, ml_conv_w, ml_conv_b, ml_gate_b, ml_norm_w, w_out, final_norm_w):
    f = np.float32
    in_maps = _prepare(x_prompt, x_sample, ln_w, w_in, na_rpb, ml_conv_w, ml_conv_b, ml_gate_b, ml_norm_w, w_out, final_norm_w)
    nc = build_program()
    res = run_bass_kernel_spmd(nc, in_maps, core_ids=list(range(8)))
    ys = [np.asarray(res.results[c]["y"], f) for c in range(8)]
    y_prompt = np.concatenate(ys[0:4], axis=0)[None]
    y_sample = np.stack(ys[4:8], axis=0)
    return (y_prompt, y_sample)
```

```python
import contextlib
import numpy as np
import concourse.bass as bass
import concourse.mybir as mybir
from concourse.bass_utils import run_bass_kernel_spmd

F32 = mybir.dt.float32
BF16 = mybir.dt.bfloat16
AF = mybir.ActivationFunctionType
ALU = mybir.AluOpType

ENGS = ("pe", "act", "dve", "pool", "sp")
NDSEM = 8

D = 2048
NTOK = 2048
NEG = -30000.0
EPS = 1e-6
NGRP = 73
SLOT = 4232
SPECIAL = [(0, 0), (0, 1), (0, 4), (0, 5), (1, 1), (1, 5), (14, 14), (14, 18), (15, 14), (15, 15), (15, 18), (15, 19)]


class _Op:
    __slots__ = ("eng", "fn", "deps", "is_dma", "idx", "signal", "cnt", "dq", "dqi", "cc")

    def __init__(self, eng, fn, is_dma):
        self.eng = eng
        self.fn = fn
        self.deps = []
        self.is_dma = is_dma
        self.signal = False
        self.cnt = 0
        self.dq = None
        self.dqi = 0
        self.cc = None


class Prog:
    def __init__(self, nc):
        self.nc = nc
        self.ops = []
        self.last_w = {}
        self.readers = {}
        self.ndma = {e: 0 for e in ENGS}
        self.ncc = 0
        self.marks = []

    def mark(self, name):
        self.marks.append((name, len(self.ops)))

    def _add(self, op, reads, writes):
        deps = set()
        for t in list(reads) + list(writes):
            w = self.last_w.get(t)
            if w is not None:
                deps.add(w)
        for t in writes:
            for r in self.readers.get(t, ()):
                deps.add(r)
        deps.discard(op)
        op.deps = [d for d in deps
                   if not (d.eng == "pe" and op.eng == "pe" and not d.is_dma and not op.is_dma)]
        for d in op.deps:
            d.signal = True
        for t in writes:
            self.last_w[t] = op
            self.readers[t] = []
        for t in reads:
            self.readers.setdefault(t, []).append(op)
        op.idx = len(self.ops)
        self.ops.append(op)
        return op

    def op(self, eng, fn, reads=(), writes=()):
        return self._add(_Op(eng, fn, False), reads, writes)

    def wait_all(self, eng, reads):
        return self._add(_Op(eng, None, False), reads, ())

    def dma(self, eng, out, in_, reads=(), writes=(), **kw):
        return self.dma_fn(eng, lambda e: e.dma_start(out=out, in_=in_, **kw), reads, writes)

    def dma_fn(self, eng, fn, reads=(), writes=()):
        o = _Op(eng, fn, True)
        o.dq = eng
        o.dqi = self.ndma[eng]
        self.ndma[eng] += 1
        o.signal = True
        return self._add(o, reads, writes)

    def cc_fn(self, eng, fn, reads=(), writes=()):
        o = _Op(eng, fn, False)
        o.cc = self.ncc
        self.ncc += 1
        return self._add(o, reads, writes)

    def barrier(self, skip=()):
        deps = []
        for e in ENGS:
            for o in reversed(self.ops):
                if o.eng == e and not o.is_dma and o.fn is not None and o.cc is None:
                    deps.append(o)
                    break
        cnt = {}
        for o in reversed(self.ops):
            if o.is_dma and cnt.get(o.dq, 0) < NDSEM:
                cnt[o.dq] = cnt.get(o.dq, 0) + 1
                if o not in skip:
                    deps.append(o)
        for d in deps:
            d.signal = True
        for e in ENGS:
            w = _Op(e, None, False)
            w.deps = list(deps)
            w.idx = len(self.ops)
            self.ops.append(w)

    def emit(self):
        nc = self.nc
        with contextlib.ExitStack() as es:
            csem = {e: es.enter_context(nc.semaphore("c_" + e)) for e in ENGS}
            dsem = {e: [es.enter_context(nc.semaphore("d_%s%d" % (e, i))) for i in range(NDSEM)]
                    for e in ENGS if self.ndma[e] > 0}
            ccsem = [es.enter_context(nc.semaphore("cc%d" % i)) for i in range(self.ncc)]
            cc = {e: 0 for e in ENGS}
            for o in self.ops:
                if o.is_dma or o.fn is None or o.cc is not None:
                    continue
                if o.signal:
                    cc[o.eng] += 1
                    o.cnt = cc[o.eng]
            block = es.enter_context(nc.Block())
            ops = self.ops

            def mk(ename):
                def body(eng):
                    waited = {}
                    for o in ops:
                        if o.eng != ename:
                            continue
                        need = {}
                        for d in o.deps:
                            if d.fn is None:
                                continue
                            if d.cc is not None:
                                key = ("cc", d.cc, 0)
                                val = 1
                            elif d.is_dma:
                                key = ("d", d.dq, d.dqi % NDSEM)
                                val = 16 * (d.dqi // NDSEM + 1)
                            else:
                                key = ("c", d.eng)
                                val = d.cnt
                            if need.get(key, 0) < val:
                                need[key] = val
                        if o.is_dma and o.dqi >= NDSEM:
                            key = ("d", o.dq, o.dqi % NDSEM)
                            val = 16 * (o.dqi // NDSEM)
                            if need.get(key, 0) < val:
                                need[key] = val
                        for key, val in need.items():
                            if waited.get(key, 0) >= val:
                                continue
                            waited[key] = val
                            if key[0] == "c":
                                sem = csem[key[1]]
                            elif key[0] == "cc":
                                sem = ccsem[key[1]]
                            else:
                                sem = dsem[key[1]][key[2]]
                            eng.wait_ge(sem, val)
                        if o.fn is None:
                            continue
                        ins = o.fn(eng)
                        if o.cc is not None:
                            ins.then_inc(ccsem[o.cc])
                        elif o.is_dma:
                            ins.then_inc(dsem[o.dq][o.dqi % NDSEM], 16)
                        elif o.signal:
                            ins.then_inc(csem[ename], 1)
                return body

            block.tensor(mk("pe"))
            block.scalar(mk("act"))
            block.vector(mk("dve"))
            block.gpsimd(mk("pool"))
            block.sync(mk("sp"))


class Arena:
    def __init__(self, t, nbytes):
        self.t = t
        self.nbytes = nbytes
        self.off = 0

    def alloc(self, shape, dt):
        esz = 4 if dt == F32 else 2
        n = 1
        for s in shape:
            n *= s
        nb = (n * esz + 31) // 32 * 32
        assert self.off + nb <= self.nbytes, ("arena overflow", self.off, nb, self.nbytes)
        ap = self.t[:, self.off // 4:(self.off + nb) // 4]
        if dt != F32:
            ap = ap.bitcast(dt)
        ap = ap[:, 0:n]
        if len(shape) == 2:
            ap = ap.rearrange("p (a b) -> p a b", a=shape[0])
        elif len(shape) == 3:
            ap = ap.rearrange("p (a b c) -> p a b c", a=shape[0], b=shape[1])
        self.off += nb
        return ap

    def mark(self):
        return self.off

    def release(self, m):
        self.off = m


ARENA_BYTES = 212736


def build_program(stop_after=None):
    nc = bass.Bass("TRN2", target_bir_lowering=False)

    def din(name, shape, dt=F32):
        return nc.dram_tensor(name, shape, dt, kind="ExternalInput").ap()

    xe = din("xe", [2560, D])
    xo_d = din("xo", [3, 2048, D])
    xoh_d = din("xoh", [3, 128, D])
    lnw = din("lnw", [128, D])
    fnw = din("fnw", [128, D])
    w_in = din("w_in", [NGRP, 128, 2048])
    w_out = din("w_out", [128, 16 * 2048])
    convw = din("convw", [128, 16 * 5])
    convb = din("convb", [128, 16])
    gateb = din("gateb", [128, 256])
    mnw = din("mnw", [128, 8])
    ident_d = din("ident", [128, 128])
    tri_d = din("tri", [128, 256])
    bg_d = din("bg", [8, 128, 640])
    bf_d = din("bf", [8, 128, 896])
    msk_d = din("msk", [128, 12 * 128 + 4])
    cf_d = din("cf", [128, 64])
    y_out = nc.dram_tensor("y", [NTOK, D], F32, kind="ExternalOutput").ap()
    mixT = nc.dram_tensor("mixT", [16, 128, 16, 128], BF16).ap()
    cin_s = nc.dram_tensor("cin_s", [8, 128, 528], F32).ap()

    P = Prog(nc)
    with contextlib.ExitStack() as es:
        arena_t = es.enter_context(nc.sbuf_tensor("arena", [128, ARENA_BYTES // 4], F32))
        A = Arena(arena_t, ARENA_BYTES)
        psb = [es.enter_context(nc.psum_tensor("ps%d" % i, [128, 512], F32)) for i in range(8)]

        def PS(i):
            return psb[i][:]

        def PSB(i):
            return psb[i][:].bitcast(BF16)

        def pst(i):
            return ("ps", i)

        ident = A.alloc([128], F32)
        identb = A.alloc([128], BF16)
        tri = A.alloc([256], F32)
        onesb = A.alloc([128], BF16)
        onesf = A.alloc([128], F32)
        cw = A.alloc([16, 5], F32)
        cb = A.alloc([16], F32)
        gb = A.alloc([256], F32)
        mnw_s = A.alloc([8], F32)
        msk = A.alloc([12 * 128 + 4], F32)
        cf = A.alloc([64], F32)
        epsc = A.alloc([8], F32)
        pre_hn_mark = A.mark()
        hnT = A.alloc([16, 2048], BF16)
        hhalo = A.alloc([16, 4], BF16)
        WP = A.alloc([16, 8], F32)
        KH = A.alloc([16, 8], F32)
        ENB = A.alloc([16, 8], F32)
        EG = A.alloc([16, 8], F32)
        WSEG = A.alloc([16, 8], F32)
        GSEG = A.alloc([8], F32)
        GO = A.alloc([3, 8], F32)
        wbufs = [A.alloc([16, 128], BF16) for _ in range(4)]
        pre_halo_mark = A.mark()
        hnTh = A.alloc([16, 512], BF16)
        base_mark = A.mark()

        for nm, dst, src in (("ident", ident, ident_d), ("tri", tri, tri_d), ("cw", cw.rearrange("p a b -> p (a b)"), convw),
                             ("cb", cb, convb), ("gb", gb, gateb), ("mnw", mnw_s, mnw), ("msk", msk, msk_d),
                             ("cf", cf, cf_d)):
            P.dma("sp", dst, src, writes=[nm])
        P.op("dve", lambda e: e.tensor_copy(out=identb, in_=ident), reads=["ident"], writes=["identb"])
        P.op("pool", lambda e: e.memset(onesb, 1.0), writes=["onesb"])
        P.op("pool", lambda e: e.memset(onesf, 1.0), writes=["onesf"])
        P.op("pool", lambda e: e.memset(epsc, EPS), writes=["epsc"])

        wstate = {"n": 0}

        def load_w(g):
            if wstate.get("pre") is not None and wstate["pre"][0] == g:
                i = wstate["pre"][1]
                wstate["pre"] = None
                return i
            assert wstate.get("pre") is None, ("prefetched group not consumed", wstate.get("pre"), g)
            i = wstate["n"] % 4
            wstate["n"] += 1
            P.dma("pool", wbufs[i].rearrange("p a b -> p (a b)"), w_in[g], writes=[("wbuf", i)])
            return i

        def prefetch_w(g):
            i = load_w(g)
            wstate["pre"] = (g, i)

        def proj_mm(wi, rhs_fn, n, bank, ncols=128):
            for kc in range(16):
                P.op("pe", lambda e, kc=kc: e.matmul(out=PS(bank)[0:ncols, 0:n], lhsT=wbufs[wi][:, kc, 0:ncols],
                                                      rhs=rhs_fn(kc), start=(kc == 0), stop=(kc == 15)),
                     reads=[("wbuf", wi), "hnT"], writes=[pst(bank)])

        pbank = {"n": 0}

        def next_bank(lo, cnt):
            b = lo + pbank["n"] % cnt
            pbank["n"] += 1
            return b

        def rhs_main(tt):
            return lambda kc: hnT[:, kc, tt * 512:(tt + 1) * 512]

        def norm_phase(tiles):
            m0 = A.mark()
            lw = A.alloc([D], F32)
            xts = [A.alloc([D], F32) for _ in range(4)]
            junk = A.alloc([D], BF16)
            hns = [A.alloc([D], BF16) for _ in range(2)]
            sss = [A.alloc([2], F32) for _ in range(4)]
            P.dma("sp", lw, lnw, writes=["lw"])
            nt_ = len(tiles)

            def st1(tl):
                i4 = tl % 4
                xt, ss = xts[i4], sss[i4]
                P.dma("sp" if tl % 2 == 0 else "pool", xt, tiles[tl][0], writes=[("xt", i4)])
                P.op("act", lambda e, xt=xt, ss=ss, junk=junk: e.activation(out=junk, in_=xt, func=AF.Square, accum_out=ss[:, 0:1]),
                     reads=[("xt", i4)], writes=["junk", ("ss", i4)])

            def st2_sqrt(tl):
                i4 = tl % 4
                ss = sss[i4]
                P.op("act", lambda e, ss=ss: e.activation(out=ss[:, 1:2], in_=ss[:, 0:1], func=AF.Sqrt, scale=1.0 / D, bias=epsc[:, 0:1]),
                     reads=[("ss", i4), "epsc"], writes=[("ss1", i4)])

            def st2_recip(tl):
                i4 = tl % 4
                ss = sss[i4]
                P.op("dve", lambda e, ss=ss: e.reciprocal(out=ss[:, 1:2], in_=ss[:, 1:2]), reads=[("ss1", i4)], writes=[("ss1", i4)])

            def st3(tl):
                i4, i = tl % 4, tl % 2
                xt, hn, ss = xts[i4], hns[i], sss[i4]
                P.op("dve", lambda e, xt=xt, ss=ss, hn=hn, lw=lw: e.scalar_tensor_tensor(out=hn, in0=xt, scalar=ss[:, 1:2], in1=lw,
                                                                                            op0=ALU.mult, op1=ALU.mult),
                     reads=[("xt", i4), ("ss1", i4), "lw"], writes=[("hn", i)])
                b0 = 4 * i
                for kc in range(16):
                    bank = b0 + kc // 8
                    P.op("pe", lambda e, kc=kc, bank=bank, hn=hn: e.transpose(
                        out=PSB(bank)[:, (kc % 8) * 128:(kc % 8 + 1) * 128], in_=hn[:, kc * 128:(kc + 1) * 128], identity=identb),
                        reads=[("hn", i), "identb"], writes=[pst(bank)])

            def st4(tl, hb):
                i = tl % 2
                b0 = 4 * i
                dst = tiles[tl][1](hb)
                src = PSB(b0 + hb)[:, 0:1024].rearrange("p (a b) -> p a b", a=8)
                if hb == 0:
                    P.op("act", lambda e, dst=dst, src=src: e.copy(out=dst, in_=src), reads=[pst(b0 + hb)], writes=["hnT"])
                else:
                    P.op("dve", lambda e, dst=dst, src=src: e.tensor_copy(out=dst, in_=src), reads=[pst(b0 + hb)], writes=["hnT"])

            for k in range(nt_ + 3):
                if k < nt_:
                    st1(k)
                if 0 <= k - 1 < nt_:
                    st2_sqrt(k - 1)
                if 0 <= k - 2 < nt_:
                    st3(k - 2)
                if 0 <= k - 3 < nt_:
                    st4(k - 3, 0)
                    st4(k - 3, 1)
                if 0 <= k - 1 < nt_:
                    st2_recip(k - 1)
            prefetch_w(72)
            P.barrier()
            A.release(m0)

        def gates_pass(next_group):
            m0 = A.mark()
            gT = A.alloc([2048], F32)
            LI = A.alloc([16, 8], F32)
            LF = A.alloc([16, 8], F32)
            Bc = A.alloc([16, 8], F32)
            Gt = A.alloc([16, 8], F32)
            ROFF = A.alloc([16, 8], F32)
            tmpg = A.alloc([16, 8], F32)
            gtok = A.alloc([16, 16], F32)
            wi = load_w(72)
            for tt in range(4):
                bank = next_bank(0, 4)
                proj_mm(wi, rhs_main(tt), 512, bank, ncols=16)
                P.op("act", lambda e, bank=bank, tt=tt: e.copy(out=gT[0:16, tt * 512:(tt + 1) * 512], in_=PS(bank)[0:16, 0:512]),
                     reads=[pst(bank)], writes=["gT"])
            for tl in range(16):
                P.op("pe", lambda e, tl=tl: e.transpose(out=PS(4)[:, tl * 16:(tl + 1) * 16], in_=gT[0:16, tl * 128:(tl + 1) * 128],
                                                        identity=ident[0:16, 0:16]),
                     reads=["gT", "ident"], writes=[pst(4)])
            P.op("dve", lambda e: e.tensor_tensor(out=gtok.rearrange("p a b -> p (a b)"), in0=PS(4)[:, 0:256], in1=gb, op=ALU.add),
                 reads=[pst(4), "gb"], writes=["gtok"])
            g5 = gtok.rearrange("p n (d t h) -> p n d t h", d=2, t=2)
            for d in range(2):
                P.op("dve", lambda e, d=d: e.tensor_copy(out=LI[:, :, d * 4:(d + 1) * 4], in_=g5[:, :, d, 0, :]),
                     reads=["gtok"], writes=["LI"])
                P.op("act", lambda e, d=d: e.activation(out=LF[:, :, d * 4:(d + 1) * 4], in_=g5[:, :, d, 1, :], func=AF.Exp, scale=-1.0),
                     reads=["gtok"], writes=["LF"])
            P.op("act", lambda e: e.activation(out=LF, in_=LF, func=AF.Ln, bias=1.0), reads=["LF"], writes=["LF"])
            P.op("dve", lambda e: e.tensor_scalar(out=LF, in0=LF, scalar1=-1.0, scalar2=None, op0=ALU.mult), reads=["LF"], writes=["LF"])
            P.op("pe", lambda e: e.matmul(out=PS(5)[:, 0:128], lhsT=tri[:, 0:128], rhs=LF.rearrange("p a b -> p (a b)"),
                                          start=True, stop=True), reads=["LF", "tri"], writes=[pst(5)])
            P.op("pe", lambda e: e.matmul(out=PS(7)[:, 0:128], lhsT=tri[:, 128:256], rhs=LF.rearrange("p a b -> p (a b)"),
                                          start=True, stop=True), reads=["LF", "tri"], writes=[pst(7)])
            P.op("pe", lambda e: e.matmul(out=PS(6)[:, 0:128], lhsT=onesf, rhs=LF.rearrange("p a b -> p (a b)"), start=True, stop=True),
                 reads=["LF", "onesf"], writes=[pst(6)])
            for d in range(2):
                bk = 5 if d == 0 else 7
                P.op("dve", lambda e, d=d, bk=bk: e.tensor_copy(out=Bc[:, :, d * 4:(d + 1) * 4],
                                                                in_=PS(bk)[:, 0:128].rearrange("p (a b) -> p a b", a=16)[:, :, d * 4:(d + 1) * 4]),
                     reads=[pst(bk)], writes=["Bc"])
            P.op("dve", lambda e: e.tensor_copy(out=Gt.rearrange("p a b -> p (a b)"), in_=PS(6)[:, 0:128]), reads=[pst(6)], writes=["Gt"])
            P.op("pool", lambda e: e.memset(ROFF, 0.0), writes=["ROFF"])
            for n in range(14, -1, -1):
                P.op("dve", lambda e, n=n: e.tensor_tensor(out=ROFF[:, n, 0:4], in0=ROFF[:, n + 1, 0:4], in1=Gt[:, n + 1, 0:4], op=ALU.add),
                     reads=["ROFF", "Gt"], writes=["ROFF"])
            for n in range(1, 16):
                P.op("dve", lambda e, n=n: e.tensor_tensor(out=ROFF[:, n, 4:8], in0=ROFF[:, n - 1, 4:8], in1=Gt[:, n - 1, 4:8], op=ALU.add),
                     reads=["ROFF", "Gt"], writes=["ROFF"])
            P.op("dve", lambda e: e.tensor_tensor(out=GSEG[:, 0:4], in0=ROFF[:, 0, 0:4], in1=Gt[:, 0, 0:4], op=ALU.add),
                 reads=["ROFF", "Gt"], writes=["GSEG"])
            P.op("dve", lambda e: e.tensor_tensor(out=GSEG[:, 4:8], in0=ROFF[:, 15, 4:8], in1=Gt[:, 15, 4:8], op=ALU.add),
                 reads=["ROFF", "Gt"], writes=["GSEG"])
            P.op("dve", lambda e: e.tensor_tensor(out=tmpg, in0=LI, in1=Bc, op=ALU.subtract), reads=["LI", "Bc"], writes=["tmpg"])
            P.op("act", lambda e: e.activation(out=WP, in_=tmpg, func=AF.Exp), reads=["tmpg"], writes=["WP"])
            P.op("act", lambda e: e.activation(out=EG, in_=Gt, func=AF.Exp), reads=["Gt"], writes=["EG"])
            P.op("act", lambda e: e.activation(out=ENB, in_=Bc, func=AF.Exp, scale=-1.0), reads=["Bc"], writes=["ENB"])
            P.op("dve", lambda e: e.tensor_tensor(out=KH, in0=WP, in1=EG, op=ALU.mult), reads=["WP", "EG"], writes=["KH"])
            P.op("act", lambda e: e.activation(out=tmpg, in_=ROFF, func=AF.Exp), reads=["ROFF", "WP"], writes=["tmpg"])
            P.op("dve", lambda e: e.tensor_tensor(out=WSEG, in0=KH, in1=tmpg, op=ALU.mult), reads=["KH", "tmpg"], writes=["WSEG"])
            prefetch_w(next_group)
            return m0

        def conv_proj(g, cin, ctok, halo_fn):
            wi = load_w(g)
            for tt in range(4):
                bank = next_bank(0, 4)
                proj_mm(wi, rhs_main(tt), 512, bank)
                P.op("act", lambda e, bank=bank, tt=tt, cin=cin: e.copy(out=cin[:, 2 + tt * 512:2 + (tt + 1) * 512], in_=PS(bank)[:, 0:512]),
                     reads=[pst(bank)], writes=[ctok])
            bank = next_bank(0, 4)
            proj_mm(wi, halo_fn, 4, bank)
            P.op("act", lambda e, bank=bank, cin=cin: e.copy(out=cin[:, 0:2], in_=PS(bank)[:, 0:2]), reads=[pst(bank)], writes=[ctok])
            P.op("act", lambda e, bank=bank, cin=cin: e.copy(out=cin[:, 2050:2052], in_=PS(bank)[:, 2:4]), reads=[pst(bank)], writes=[ctok])

        def conv_post(cgi, dst, dtok, post_scale, cin, ctok, cacc):
            P.op("dve", lambda e, cin=cin, cacc=cacc: e.tensor_scalar(out=cacc, in0=cin[:, 0:2048], scalar1=cw[:, cgi, 0:1], scalar2=None, op0=ALU.mult),
                 reads=[ctok, "cw"], writes=["cacc"])
            for j in range(1, 5):
                P.op("dve", lambda e, j=j, cin=cin, cacc=cacc: e.scalar_tensor_tensor(out=cacc, in0=cin[:, j:j + 2048], scalar=cw[:, cgi, j:j + 1],
                                                                                      in1=cacc, op0=ALU.mult, op1=ALU.add),
                     reads=[ctok, "cw", "cacc"], writes=["cacc"])
            if post_scale is None:
                P.op("act", lambda e, cacc=cacc: e.activation(out=dst, in_=cacc, func=AF.Silu, bias=cb[:, cgi:cgi + 1]),
                     reads=["cacc", "cb"], writes=[dtok])
            else:
                P.op("act", lambda e, cacc=cacc: e.activation(out=cacc, in_=cacc, func=AF.Silu, bias=cb[:, cgi:cgi + 1]),
                     reads=["cacc", "cb"], writes=["cacc"])
                P.op("dve", lambda e, cacc=cacc: e.tensor_scalar(out=dst, in0=cacc, scalar1=post_scale, scalar2=None, op0=ALU.mult),
                     reads=["cacc"], writes=[dtok])

        def transposes_to_tok(srcT, dst_fn, tag, stok="convdst"):
            for n4 in range(4):
                bank = next_bank(4, 4)
                for nn in range(4):
                    n = n4 * 4 + nn
                    for ec in range(2):
                        P.op("pe", lambda e, n=n, nn=nn, ec=ec, bank=bank: e.transpose(
                            out=PSB(bank)[:, nn * 256 + ec * 128: nn * 256 + (ec + 1) * 128],
                            in_=srcT[:, ec, n * 128:(n + 1) * 128], identity=identb),
                            reads=[stok, "identb"], writes=[pst(bank)])
                for nn in range(4):
                    n = n4 * 4 + nn
                    src = PSB(bank)[:, nn * 256:(nn + 1) * 256]
                    dstap = dst_fn(n)
                    if nn % 2 == 0:
                        P.op("act", lambda e, dstap=dstap, src=src: e.copy(out=dstap, in_=src), reads=[pst(bank)], writes=[tag])
                    else:
                        P.op("dve", lambda e, dstap=dstap, src=src: e.tensor_copy(out=dstap, in_=src), reads=[pst(bank)], writes=[tag])

        def kv_operands(h, kTb, ktok, vTb, vaug, cins, cacc, halo_fn, vtok="convdst"):
            def vproj(ec):
                wi = load_w(48 + 2 * h + ec)
                for tt in range(4):
                    bank = next_bank(0, 4)
                    proj_mm(wi, rhs_main(tt), 512, bank)
                    P.op("act", lambda e, bank=bank, tt=tt, ec=ec: e.copy(out=vTb[:, ec, tt * 512:(tt + 1) * 512], in_=PS(bank)[:, 0:512]),
                         reads=[pst(bank)], writes=[vtok])
            conv_proj(40 + 2 * h, cins[0], ("cin", 0), halo_fn)
            conv_proj(41 + 2 * h, cins[1], ("cin", 1), halo_fn)
            conv_post(8 + 2 * h, kTb[:, 0, :], "convdst", None, cins[0], ("cin", 0), cacc)
            vproj(0)
            conv_post(9 + 2 * h, kTb[:, 1, :], "convdst", None, cins[1], ("cin", 1), cacc)
            vproj(1)
            transposes_to_tok(kTb, lambda n: ktok[:, n, :], "ktok")
            transposes_to_tok(vTb, lambda n: vaug[:, n, 0:256], "vaug", stok=vtok)

        mX = A.mark()
        T_all = A.alloc([3, 4, 528], F32)
        P.op("pool", lambda e: e.memset(T_all, 0.0), writes=["T_all"])
        for o in range(3):
            P.mark("X%d_norm" % o)
            tiles = []
            for tl in range(16):
                tiles.append((xo_d[o, tl * 128:(tl + 1) * 128, :],
                              (lambda hb, tl=tl: hnT[:, hb * 8:(hb + 1) * 8, tl * 128:(tl + 1) * 128])))
            tiles.append((xoh_d[o], (lambda hb: hnTh[:, hb * 8:(hb + 1) * 8, 0:128])))
            norm_phase(tiles)
            P.mark("X%d_gates" % o)
            m1 = gates_pass(40)
            P.mark("X%d_heads" % o)
            P.op("dve", lambda e, o=o: e.tensor_copy(out=GO[:, o, :], in_=GSEG), reads=["GSEG"], writes=["GO"])
            wbl = A.alloc([16, 4], F32)
            cins = [A.alloc([2052], F32) for _ in range(2)]
            cacc = A.alloc([2048], F32)
            kTb = A.alloc([2, 2048], BF16)
            ktok = A.alloc([16, 256], BF16)
            vTb = A.alloc([2, 2048], BF16)
            vaug = A.alloc([16, 264], BF16)
            khs = [A.alloc([256], BF16) for _ in range(8)]
            P.op("dve", lambda e, o=o, wbl=wbl: e.tensor_scalar(out=wbl, in0=WSEG[:, :, 0:4], scalar1=cf[:, o:o + 1], scalar2=None, op0=ALU.mult),
                 reads=["WSEG", "cf"], writes=["wbl"])
            P.op("dve", lambda e, o=o, wbl=wbl: e.scalar_tensor_tensor(out=wbl, in0=WSEG[:, :, 4:8], scalar=cf[:, 3 + o:4 + o], in1=wbl,
                                                                        op0=ALU.mult, op1=ALU.add),
                 reads=["WSEG", "cf", "wbl"], writes=["wbl"])
            P.op("pool", lambda e, vaug=vaug: e.memset(vaug, 0.0), writes=["vaug"])
            P.op("pool", lambda e, vaug=vaug: e.memset(vaug[:, :, 256:257], 1.0), writes=["vaug"])
            for h in range(4):
                kv_operands(h, kTb, ktok, vTb, vaug, cins, cacc, lambda kc: hnTh[:, kc, 0:4])
                def kscale(n):
                    kb = khs[n % 8]
                    P.op("act", lambda e, n=n, h=h, kb=kb, ktok=ktok, wbl=wbl: e.activation(out=kb, in_=ktok[:, n, :], func=AF.Copy,
                                                                                          scale=wbl[:, n, h:h + 1]),
                         reads=["ktok", "wbl"], writes=[("khs", n % 8)])
                for n in range(8):
                    kscale(n)
                for n in range(16):
                    kb = khs[n % 8]
                    if n >= 1 and n + 7 < 16:
                        kscale(n + 7)
                    for ec in range(2):
                        P.op("pe", lambda e, n=n, ec=ec, kb=kb, vaug=vaug: e.matmul(out=PS(ec)[:, 0:257], lhsT=kb[:, ec * 128:(ec + 1) * 128],
                                                                                    rhs=vaug[:, n, 0:257], start=(n == 0), stop=(n == 15)),
                             reads=[("khs", n % 8), "vaug"], writes=[pst(ec)])
                for ec in range(2):
                    P.op("dve", lambda e, ec=ec, o=o, h=h: e.tensor_copy(out=T_all[:, o, h, ec * 264:ec * 264 + 257], in_=PS(ec)[:, 0:257]),
                         reads=[pst(ec)], writes=["T_all"])
            P.barrier()
            A.release(m1)
        P.mark("X_combine")
        COEF = A.alloc([3, 8], F32)
        cacc2 = A.alloc([3, 4], F32)
        cstg = A.alloc([8, 528], F32)
        for d in range(2):
            for o in range(3):
                for u in range(3):
                    col = 6 + d * 9 + o * 3 + u
                    if u == 0:
                        P.op("dve", lambda e, o=o, u=u, col=col, d=d: e.tensor_scalar(out=cacc2[:, o, :], in0=GO[:, u, d * 4:(d + 1) * 4],
                                                                                     scalar1=cf[:, col:col + 1], scalar2=None, op0=ALU.mult),
                             reads=["GO", "cf"], writes=["cacc2"])
                    else:
                        P.op("dve", lambda e, o=o, u=u, col=col, d=d: e.scalar_tensor_tensor(out=cacc2[:, o, :], in0=GO[:, u, d * 4:(d + 1) * 4],
                                                                                            scalar=cf[:, col:col + 1], in1=cacc2[:, o, :],
                                                                                            op0=ALU.mult, op1=ALU.add),
                             reads=["GO", "cf", "cacc2"], writes=["cacc2"])
            P.op("act", lambda e: e.activation(out=cacc2, in_=cacc2, func=AF.Exp), reads=["cacc2"], writes=["cacc2"])
            for o in range(3):
                col = d * 3 + o
                P.op("dve", lambda e, o=o, col=col, d=d: e.tensor_scalar(out=COEF[:, o, d * 4:(d + 1) * 4], in0=cacc2[:, o, :],
                                                                        scalar1=cf[:, col:col + 1], scalar2=None, op0=ALU.mult),
                     reads=["cacc2", "cf"], writes=["COEF"])
        for d in range(2):
            for h in range(4):
                c = d * 4 + h
                for o in range(3):
                    if o == 0:
                        P.op("dve", lambda e, c=c, o=o, h=h: e.tensor_scalar(out=cstg[:, c, :], in0=T_all[:, o, h, :], scalar1=COEF[:, o, c:c + 1],
                                                                            scalar2=None, op0=ALU.mult),
                             reads=["T_all", "COEF"], writes=["cstg"])
                    else:
                        P.op("dve", lambda e, c=c, o=o, h=h: e.scalar_tensor_tensor(out=cstg[:, c, :], in0=T_all[:, o, h, :], scalar=COEF[:, o, c:c + 1],
                                                                                   in1=cstg[:, c, :], op0=ALU.mult, op1=ALU.add),
                             reads=["T_all", "COEF", "cstg"], writes=["cstg"])
        P.dma("sp", cin_s.rearrange("c p f -> p c f"), cstg, reads=["cstg"], writes=["cin_s"])
        P.barrier()
        A.release(mX)
        if stop_after == "X":
            P.emit()
            return nc

        tiles = []
        for tl in range(20):
            if tl < 16:
                tiles.append((xe[tl * 128:(tl + 1) * 128, :], (lambda hb, tl=tl: hnT[:, hb * 8:(hb + 1) * 8, tl * 128:(tl + 1) * 128])))
            else:
                tiles.append((xe[tl * 128:(tl + 1) * 128, :], (lambda hb, tl=tl: hnTh[:, hb * 8:(hb + 1) * 8, (tl - 16) * 128:(tl - 15) * 128])))
        P.mark("A_norm")
        norm_phase(tiles)
        P.op("dve", lambda e: e.tensor_copy(out=hhalo, in_=hnTh[:, :, 254:258]), reads=["hnT"], writes=["hhalo"])
        P.mark("A_gates")
        mN = gates_pass(0)
        P.mark("N")

        QTs = [A.alloc([2048], BF16) for _ in range(2)]
        KTs = [A.alloc([2560], BF16) for _ in range(2)]
        VTs = [A.alloc([2560], BF16) for _ in range(2)]
        GTs = [A.alloc([2048], BF16) for _ in range(2)]
        Vtoks = [A.alloc([20, 128], BF16) for _ in range(2)]
        BGs = [A.alloc([640], F32) for _ in range(2)]
        BFs = [A.alloc([896], F32) for _ in range(2)]
        attT = A.alloc([2048], BF16)
        scs = [A.alloc([768], F32) for _ in range(2)]
        PTs = [A.alloc([768], BF16) for _ in range(2)]
        rden = [A.alloc([128], F32) for _ in range(2)]
        atmp = [A.alloc([128], F32) for _ in range(2)]

        def proj_gen(h):
            p = h % 2
            QT, KT, VT, GTb, Vtok = QTs[p], KTs[p], VTs[p], GTs[p], Vtoks[p]
            P.dma("sp", BGs[p], bg_d[h], writes=[("BG", p)])
            P.dma("sp", BFs[p], bf_d[h], writes=[("BF", p)])
            wi = load_w(h)
            for tt in range(4):
                bank = next_bank(0, 2)
                proj_mm(wi, rhs_main(tt), 512, bank)
                yield
                P.op("act", lambda e, bank=bank, tt=tt, QT=QT: e.activation(out=QT[:, tt * 512:(tt + 1) * 512], in_=PS(bank)[:, 0:512],
                                                                            func=AF.Copy, scale=128.0 ** -0.5),
                     reads=[pst(bank)], writes=[("QT", p)])
                yield
            for g, dst, tok in ((8 + h, KT, ("KT", p)), (16 + h, VT, ("VT", p))):
                wi = load_w(g)
                bank = next_bank(0, 2)
                proj_mm(wi, lambda kc: hnTh[:, kc, 0:512], 512, bank)
                yield
                P.op("act", lambda e, bank=bank, dst=dst: e.copy(out=dst[:, 0:256], in_=PS(bank)[:, 0:256]), reads=[pst(bank)], writes=[tok])
                P.op("dve", lambda e, bank=bank, dst=dst: e.tensor_copy(out=dst[:, 2304:2560], in_=PS(bank)[:, 256:512]), reads=[pst(bank)], writes=[tok])
                yield
                for tt in range(4):
                    bank = next_bank(0, 2)
                    proj_mm(wi, rhs_main(tt), 512, bank)
                    yield
                    if tt % 2 == 0:
                        P.op("act", lambda e, bank=bank, tt=tt, dst=dst: e.copy(out=dst[:, 256 + tt * 512:256 + (tt + 1) * 512], in_=PS(bank)[:, 0:512]),
                             reads=[pst(bank)], writes=[tok])
                    else:
                        P.op("dve", lambda e, bank=bank, tt=tt, dst=dst: e.tensor_copy(out=dst[:, 256 + tt * 512:256 + (tt + 1) * 512], in_=PS(bank)[:, 0:512]),
                             reads=[pst(bank)], writes=[tok])
                    yield
            for t4 in range(5):
                bank = next_bank(0, 2)
                for tq in range(4):
                    t = t4 * 4 + tq
                    P.op("pe", lambda e, t=t, tq=tq, bank=bank, VT=VT: e.transpose(out=PSB(bank)[:, tq * 128:(tq + 1) * 128],
                                                                                   in_=VT[:, t * 128:(t + 1) * 128], identity=identb),
                         reads=[("VT", p), "identb"], writes=[pst(bank)])
                yield
                P.op("dve", lambda e, t4=t4, bank=bank, Vtok=Vtok: e.tensor_copy(out=Vtok[:, t4 * 4:(t4 + 1) * 4, :],
                                                                                in_=PSB(bank)[:, 0:512].rearrange("p (a b) -> p a b", a=4)),
                     reads=[pst(bank)], writes=[("Vtok", p)])
                yield

            wi = load_w(24 + h)
            for tt in range(4):
                bank = next_bank(0, 2)
                proj_mm(wi, rhs_main(tt), 512, bank)
                yield
                P.op("act", lambda e, bank=bank, tt=tt, GTb=GTb: e.activation(out=GTb[:, tt * 512:(tt + 1) * 512], in_=PS(bank)[:, 0:512], func=AF.Silu),
                     reads=[pst(bank)], writes=[("GT", p)])
                yield

        def tiles_of(b):
            if b == 0:
                return [0, 1, 2, 3, 4, 5]
            if b == 15:
                return [14, 15, 16, 17, 18, 19]
            return [b + j for j in range(5)]

        def scores(h, b):
            p = h % 2
            i2 = b % 2
            sbank = [2 + 2 * i2, 3 + 2 * i2]
            for i, t in enumerate(tiles_of(b)):
                bk = sbank[i // 4]
                P.op("pe", lambda e, t=t, i=i, bk=bk, b=b, p=p: e.matmul(out=PS(bk)[:, (i % 4) * 128:(i % 4 + 1) * 128],
                                                                         lhsT=KTs[p][:, t * 128:(t + 1) * 128],
                                                                         rhs=QTs[p][:, b * 128:(b + 1) * 128], start=True, stop=True),
                     reads=[("KT", p), ("QT", p)], writes=[pst(bk)])

        def softmax(h, b):
            p = h % 2
            BG, BFt = BGs[p], BFs[p]
            i2 = b % 2
            sc, PT = scs[i2], PTs[i2]
            sbank = [2 + 2 * i2, 3 + 2 * i2]
            tiles = tiles_of(b)
            nt = len(tiles)
            if b not in (0, 1, 14, 15):
                P.op("dve", lambda e, sc=sc, sbank=sbank, BG=BG: e.tensor_tensor(out=sc[:, 0:512], in0=PS(sbank[0])[:, 0:512], in1=BG[:, 0:512], op=ALU.add),
                     reads=[pst(sbank[0]), ("BG", p)], writes=[("sc", i2)])
                P.op("dve", lambda e, sc=sc, sbank=sbank, BG=BG: e.tensor_tensor(out=sc[:, 512:640], in0=PS(sbank[1])[:, 0:128], in1=BG[:, 512:640], op=ALU.add),
                     reads=[pst(sbank[1]), ("BG", p)], writes=[("sc", i2)])
            else:
                for i, t in enumerate(tiles):
                    bk = sbank[i // 4]
                    jr = t - b
                    if (b, t) in SPECIAL:
                        tab = BFt[:, (jr + 1) * 128:(jr + 2) * 128]
                    else:
                        tab = BG[:, jr * 128:(jr + 1) * 128]
                    P.op("dve", lambda e, sc=sc, i=i, bk=bk, tab=tab: e.tensor_tensor(out=sc[:, i * 128:(i + 1) * 128],
                                                                                     in0=PS(bk)[:, (i % 4) * 128:(i % 4 + 1) * 128],
                                                                                     in1=tab, op=ALU.add),
                         reads=[pst(bk), ("BG", p), ("BF", p)], writes=[("sc", i2)])
                    if (b, t) in SPECIAL:
                        mi = SPECIAL.index((b, t))
                        P.op("dve", lambda e, sc=sc, i=i, mi=mi: e.tensor_tensor(out=sc[:, i * 128:(i + 1) * 128], in0=sc[:, i * 128:(i + 1) * 128],
                                                                                in1=msk[:, mi * 128:(mi + 1) * 128], op=ALU.add),
                             reads=[("sc", i2), "msk"], writes=[("sc", i2)])
            P.op("act", lambda e, sc=sc, PT=PT, nt=nt: e.activation(out=PT[:, 0:nt * 128], in_=sc[:, 0:nt * 128], func=AF.Exp),
                 reads=[("sc", i2)], writes=[("PT", i2)])

        def pv(h, b):
            p = h % 2
            i2 = b % 2
            PT = PTs[i2]
            tiles = tiles_of(b)
            nt = len(tiles)
            for i, t in enumerate(tiles):
                P.op("pe", lambda e, t=t, i=i, PT=PT, nt=nt, p=p: e.matmul(out=PS(6)[:, 0:128], lhsT=Vtoks[p][:, t, :], rhs=PT[:, i * 128:(i + 1) * 128],
                                                                          start=(i == 0), stop=(i == nt - 1)),
                     reads=[("Vtok", p), ("PT", i2)], writes=[pst(6)])
            for i, t in enumerate(tiles):
                P.op("pe", lambda e, i=i, PT=PT, nt=nt: e.matmul(out=PS(7)[:, 0:128], lhsT=onesb, rhs=PT[:, i * 128:(i + 1) * 128],
                                                                start=(i == 0), stop=(i == nt - 1)),
                     reads=["onesb", ("PT", i2)], writes=[pst(7)])
            rd, at = rden[i2], atmp[i2]
            P.op("act", lambda e, rd=rd: e.activation(out=rd, in_=PS(7)[:, 0:128], func=AF.Ln), reads=[pst(7)], writes=[("rd", i2)])
            P.op("act", lambda e, rd=rd: e.activation(out=rd, in_=rd, func=AF.Exp, scale=-1.0), reads=[("rd", i2)], writes=[("rd", i2)])
            P.op("dve", lambda e, rd=rd, at=at: e.tensor_tensor(out=at, in0=PS(6)[:, 0:128], in1=rd, op=ALU.mult),
                 reads=[pst(6), ("rd", i2)], writes=[("at", i2)])
            P.op("dve", lambda e, at=at, b=b, p=p: e.tensor_tensor(out=attT[:, b * 128:(b + 1) * 128], in0=at, in1=GTs[p][:, b * 128:(b + 1) * 128], op=ALU.mult),
                 reads=[("at", i2), ("GT", p)], writes=["attT"])

        for _ in proj_gen(0):
            pass
        for h in range(8):
            gen = proj_gen(h + 1) if h < 7 else iter(())
            scores(h, 0)
            scores(h, 1)
            softmax(h, 0)
            for b in range(16):
                if b + 2 < 16:
                    scores(h, b + 2)
                next(gen, None)
                if b + 1 < 16:
                    softmax(h, b + 1)
                pv(h, b)
                next(gen, None)
            for _ in gen:
                pass
            for hf in range(2):
                P.dma("sp", mixT[hf * 8:(hf + 1) * 8, :, h, :].rearrange("a p t -> p a t"),
                      attT[:, hf * 1024:(hf + 1) * 1024].rearrange("p (a t) -> p a t", a=8), reads=["attT"], writes=[("mixT", h)])
        prefetch_w(40)
        P.barrier()
        A.release(pre_halo_mark)
        if stop_after == "N":
            P.emit()
            return nc

        P.mark("M")
        mM = A.mark()
        qT = A.alloc([2, 2048], BF16)
        kT2 = A.alloc([2, 2048], BF16)
        ktok2 = A.alloc([16, 256], BF16)
        vaug2 = A.alloc([16, 264], BF16)
        goz = A.alloc([2, 2048], BF16)
        Hs = A.alloc([16, 256], F32)
        C32 = [A.alloc([2, 264], F32) for _ in range(2)]
        Cbf = [[A.alloc([2, 264], BF16) for _ in range(2)] for _ in range(2)]
        PTm = [[A.alloc([128], BF16) for _ in range(2)] for _ in range(2)]
        khm = [A.alloc([256], BF16) for _ in range(2)]
        rl = [A.alloc([2], F32) for _ in range(2)]
        tmpH = [A.alloc([256], F32) for _ in range(2)]
        hsn = A.alloc([256], BF16)
        hss = A.alloc([2], F32)
        memT = kT2
        cins2 = [A.alloc([2052], F32) for _ in range(2)]
        cacc2b = A.alloc([2048], F32)
        zbuf = A.alloc([2048], F32)
        stile = [A.alloc([512], F32) for _ in range(2)]
        vT2 = A.alloc([2, 2048], BF16)
        own_halo = lambda kc: hhalo[:, kc, 0:4]

        def gate_z(g_z):
            wi = load_w(g_z)
            for tt in range(4):
                bank = next_bank(0, 2)
                proj_mm(wi, rhs_main(tt), 512, bank)
                P.op("act", lambda e, bank=bank, tt=tt: e.activation(out=zbuf[:, tt * 512:(tt + 1) * 512], in_=PS(bank)[:, 0:512], func=AF.Silu),
                     reads=[pst(bank)], writes=["zbuf"])

        def gate_o(g_o, ec):
            wi = load_w(g_o)
            for tt in range(4):
                bank = next_bank(0, 2)
                st = stile[tt % 2]
                proj_mm(wi, rhs_main(tt), 512, bank)
                P.op("act", lambda e, bank=bank, st=st: e.activation(out=st, in_=PS(bank)[:, 0:512], func=AF.Sigmoid),
                     reads=[pst(bank)], writes=[("stile", tt % 2)])
                P.op("dve", lambda e, tt=tt, st=st: e.tensor_tensor(out=goz[:, ec, tt * 512:(tt + 1) * 512], in0=st, in1=zbuf[:, tt * 512:(tt + 1) * 512],
                                                                   op=ALU.mult),
                     reads=[("stile", tt % 2), "zbuf"], writes=["goz"])

        P.op("pool", lambda e: e.memset(vaug2, 0.0), writes=["vaug"])
        P.op("pool", lambda e: e.memset(vaug2[:, :, 256:257], 1.0), writes=["vaug"])
        for h in range(4):
            P.mark("M%d_proj" % h)
            kv_operands(h, kT2, ktok2, vT2, vaug2, cins2, cacc2b, own_halo)
            conv_proj(32 + 2 * h, cins2[0], ("cin", 0), own_halo)
            conv_proj(33 + 2 * h, cins2[1], ("cin", 1), own_halo)
            gate_z(64 + 2 * h)
            conv_post(2 * h, qT[:, 0, :], "convdst", 1.0 / 16.0, cins2[0], ("cin", 0), cacc2b)
            gate_o(56 + 2 * h, 0)
            gate_z(65 + 2 * h)
            conv_post(2 * h + 1, qT[:, 1, :], "convdst", 1.0 / 16.0, cins2[1], ("cin", 1), cacc2b)
            gate_o(57 + 2 * h, 1)
            for d in range(2):
                c = d * 4 + h
                P.dma("sp", C32[d].rearrange("p a b -> p (a b)"), cin_s[c], reads=["cin_s"], writes=[("C32", d)])
                P.op("act", lambda e, d=d: e.copy(out=Cbf[d][0], in_=C32[d]), reads=[("C32", d)], writes=[("Cbf", d, 0)])
            P.mark("M%d_scan" % h)
            def chunk_of(j, d):
                return j if d == 0 else 15 - j

            def st_scores(j, d):
                n, c, bS = chunk_of(j, d), d * 4 + h, 4 * d
                for ec in range(2):
                    P.op("pe", lambda e, n=n, ec=ec, bS=bS: e.matmul(out=PS(bS)[:, 0:128], lhsT=kT2[:, ec, n * 128:(n + 1) * 128],
                                                                     rhs=qT[:, ec, n * 128:(n + 1) * 128], start=(ec == 0), stop=(ec == 1)),
                         reads=["convdst"], writes=[pst(bS)])
                P.op("dve", lambda e, n=n, c=c, d=d, bS=bS, j=j: e.scalar_tensor_tensor(out=PTm[d][j % 2], in0=PS(bS)[:, 0:128], scalar=WP[:, n, c:c + 1],
                                                                                       in1=tri[:, d * 128:(d + 1) * 128], op0=ALU.mult, op1=ALU.mult),
                     reads=[pst(bS), "WP", "tri"], writes=[("PTm", d, j % 2)])

            def st_state(j, d):
                n, c = chunk_of(j, d), d * 4 + h
                bK0, bK1 = 4 * d + 2, 4 * d + 3
                nxt = (j + 1) % 2
                for ec, bk in ((0, bK0), (1, bK1)):
                    P.op("pe", lambda e, n=n, ec=ec, d=d, bk=bk: e.matmul(out=PS(bk)[:, 0:257], lhsT=khm[d][:, ec * 128:(ec + 1) * 128],
                                                                          rhs=vaug2[:, n, 0:257], start=True, stop=True),
                         reads=[("khm", d), "vaug"], writes=[pst(bk)])
                for ec, bk in ((0, bK0), (1, bK1)):
                    P.op("dve", lambda e, n=n, c=c, d=d, ec=ec, bk=bk: e.scalar_tensor_tensor(out=C32[d][:, ec, 0:257], in0=C32[d][:, ec, 0:257],
                                                                                             scalar=EG[:, n, c:c + 1], in1=PS(bk)[:, 0:257],
                                                                                             op0=ALU.mult, op1=ALU.add),
                         reads=[("C32", d), "EG", pst(bk)], writes=[("C32", d)])
                P.op("act", lambda e, d=d, nxt=nxt: e.copy(out=Cbf[d][nxt], in_=C32[d]), reads=[("C32", d)], writes=[("Cbf", d, nxt)])
                if j + 1 < 16:
                    n1 = chunk_of(j + 1, d)
                    P.op("act", lambda e, n1=n1, c=c, d=d: e.activation(out=khm[d], in_=ktok2[:, n1, :], func=AF.Copy, scale=KH[:, n1, c:c + 1]),
                         reads=["ktok", "KH"], writes=[("khm", d)])

            def st_out(j, d):
                n, c, bO, cur = chunk_of(j, d), d * 4 + h, 4 * d + 1, j % 2
                for ec in range(2):
                    P.op("pe", lambda e, n=n, ec=ec, d=d, bO=bO, cur=cur: e.matmul(out=PS(bO)[:, 0:257], lhsT=qT[:, ec, n * 128:(n + 1) * 128],
                                                                                   rhs=Cbf[d][cur][:, ec, 0:257], start=(ec == 0), stop=False),
                         reads=["convdst", ("Cbf", d, cur)], writes=[pst(bO)])
                P.op("pe", lambda e, n=n, d=d, bO=bO, cur=cur: e.matmul(out=PS(bO)[:, 0:257], lhsT=PTm[d][cur], rhs=vaug2[:, n, 0:257], start=False, stop=True),
                     reads=[("PTm", d, cur), "vaug"], writes=[pst(bO)])
                P.op("act", lambda e, d=d, bO=bO: e.activation(out=rl[d][:, 0:1], in_=PS(bO)[:, 256:257], func=AF.Abs),
                     reads=[pst(bO)], writes=[("rl", d)])
                P.op("pool", lambda e, n=n, c=c, d=d: e.tensor_scalar(out=rl[d][:, 0:1], in0=rl[d][:, 0:1], scalar1=ENB[:, n, c:c + 1],
                                                                     scalar2=None, op0=ALU.max),
                     reads=[("rl", d), "ENB"], writes=[("rl", d)])
                P.op("dve", lambda e, d=d: e.reciprocal(out=rl[d][:, 1:2], in_=rl[d][:, 0:1]), reads=[("rl", d)], writes=[("rl1", d)])
                if j < 8:
                    P.op("act", lambda e, n=n, d=d, bO=bO: e.activation(out=Hs[:, n, :], in_=PS(bO)[:, 0:256], func=AF.Copy, scale=rl[d][:, 1:2]),
                         reads=[pst(bO), ("rl1", d)], writes=[("Hs", n)])
                else:
                    P.op("act", lambda e, d=d, bO=bO: e.activation(out=tmpH[d], in_=PS(bO)[:, 0:256], func=AF.Copy, scale=rl[d][:, 1:2]),
                         reads=[pst(bO), ("rl1", d)], writes=[("tmpH", d)])
                    P.op("pool", lambda e, n=n, d=d: e.tensor_tensor(out=Hs[:, n, :], in0=Hs[:, n, :], in1=tmpH[d], op=ALU.add),
                         reads=[("tmpH", d), ("Hs", n)], writes=[("Hs", n)])

            for d in range(2):
                c = d * 4 + h
                n0 = chunk_of(0, d)
                P.op("act", lambda e, n0=n0, c=c, d=d: e.activation(out=khm[d], in_=ktok2[:, n0, :], func=AF.Copy, scale=KH[:, n0, c:c + 1]),
                     reads=["ktok", "KH"], writes=[("khm", d)])
                st_scores(0, d)
            for j in range(16):
                for d in range(2):
                    if j + 1 < 16:
                        st_scores(j + 1, d)
                    st_state(j, d)
                    st_out(j, d)
            P.mark("M%d_post" % h)
            for n in range(16):
                P.op("act", lambda e, n=n: e.activation(out=hsn, in_=Hs[:, n, :], func=AF.Square, accum_out=hss[:, 0:1]),
                     reads=[("Hs", n)], writes=["hsn", "hss"])
                P.op("dve", lambda e: e.tensor_scalar(out=hss[:, 1:2], in0=hss[:, 0:1], scalar1=1.0 / 256, scalar2=EPS, op0=ALU.mult, op1=ALU.add),
                     reads=["hss"], writes=["hss1"])
                P.op("act", lambda e: e.activation(out=hss[:, 1:2], in_=hss[:, 1:2], func=AF.Sqrt), reads=["hss1"], writes=["hss1"])
                P.op("dve", lambda e: e.reciprocal(out=hss[:, 1:2], in_=hss[:, 1:2]), reads=["hss1"], writes=["hss1"])
                P.op("dve", lambda e, n=n: e.tensor_scalar(out=hsn, in0=Hs[:, n, :], scalar1=hss[:, 1:2], scalar2=None, op0=ALU.mult),
                     reads=[("Hs", n), "hss1", "hsn"], writes=["hsn"])
                bank = 2 + n % 2
                for ec in range(2):
                    P.op("pe", lambda e, ec=ec, bank=bank: e.transpose(out=PSB(bank)[:, ec * 128:(ec + 1) * 128], in_=hsn[:, ec * 128:(ec + 1) * 128],
                                                                       identity=identb), reads=["hsn", "identb"], writes=[pst(bank)])
                for ec in range(2):
                    P.op("dve", lambda e, ec=ec, n=n, bank=bank, h=h: e.scalar_tensor_tensor(
                        out=memT[:, ec, n * 128:(n + 1) * 128], in0=PSB(bank)[:, ec * 128:(ec + 1) * 128],
                        scalar=mnw_s[:, h * 2 + ec:h * 2 + ec + 1], in1=goz[:, ec, n * 128:(n + 1) * 128], op0=ALU.mult, op1=ALU.mult),
                        reads=[pst(bank), "mnw", "goz"], writes=["convdst"])
            for ec in range(2):
                for hf in range(2):
                    P.dma("sp", mixT[hf * 8:(hf + 1) * 8, :, 8 + 2 * h + ec, :].rearrange("a p t -> p a t"),
                          memT[:, ec, hf * 1024:(hf + 1) * 1024].rearrange("p (a t) -> p a t", a=8),
                          reads=["convdst"], writes=[("mixT", 8 + 2 * h + ec)])
        P.barrier()
        A.release(pre_hn_mark)
        if stop_after == "M":
            P.emit()
            return nc

        P.mark("O")
        WO = A.alloc([16, 2048], BF16)
        fw = A.alloc([D], F32)
        mts = [A.alloc([16, 128], BF16) for _ in range(3)]
        xos = [A.alloc([D], F32) for _ in range(3)]
        ys = [A.alloc([D], F32) for _ in range(2)]
        junk2 = A.alloc([D], BF16)
        so = [A.alloc([8], F32) for _ in range(2)]
        w_out3 = w_out.rearrange("p (a b) -> p a b", a=16)
        for q4 in range(4):
            P.dma("pool", WO[:, :, q4 * 512:(q4 + 1) * 512], w_out3[:, :, q4 * 512:(q4 + 1) * 512], writes=[("WO", q4)])
        P.dma("sp", fw, fnw, writes=["fw"])
        outs = []
        for tl in range(16):
            i = tl % 2
            i3 = tl % 3
            mt, xo, yy, s8 = mts[i3], xos[i3], ys[i], so[i]
            P.dma("sp", mt.rearrange("p a b -> p (a b)"), mixT[tl].rearrange("p a b -> p (a b)"),
                  reads=[("mixT", c) for c in range(16)], writes=[("mt", i3)])
            P.dma("sp", xo, xe[tl * 128:(tl + 1) * 128, :], writes=[("xo", i3)])
            for dq in range(4):
                bank = (tl % 2) * 4 + dq
                for mc in range(16):
                    P.op("pe", lambda e, mc=mc, dq=dq, bank=bank, mt=mt: e.matmul(out=PS(bank)[:, 0:512], lhsT=mt[:, mc, :],
                                                                                  rhs=WO[:, mc, dq * 512:(dq + 1) * 512],
                                                                                  start=(mc == 0), stop=(mc == 15)),
                         reads=[("mt", i3), ("WO", dq)], writes=[pst(bank)])
                P.op("dve", lambda e, dq=dq, bank=bank, xo=xo, yy=yy: e.tensor_tensor(out=yy[:, dq * 512:(dq + 1) * 512], in0=PS(bank)[:, 0:512],
                                                                                      in1=xo[:, dq * 512:(dq + 1) * 512], op=ALU.add),
                     reads=[pst(bank), ("xo", i3)], writes=[("yy", i)])
            P.op("act", lambda e, yy=yy, s8=s8: e.activation(out=junk2, in_=yy, func=AF.Square, accum_out=s8[:, 0:1]),
                 reads=[("yy", i)], writes=["junk2", ("s8", i)])
            P.op("dve", lambda e, s8=s8: e.tensor_scalar(out=s8[:, 1:2], in0=s8[:, 0:1], scalar1=1.0 / D, scalar2=EPS, op0=ALU.mult, op1=ALU.add),
                 reads=[("s8", i)], writes=[("s81", i)])
            P.op("act", lambda e, s8=s8: e.activation(out=s8[:, 1:2], in_=s8[:, 1:2], func=AF.Sqrt), reads=[("s81", i)], writes=[("s81", i)])
            P.op("dve", lambda e, s8=s8: e.reciprocal(out=s8[:, 1:2], in_=s8[:, 1:2]), reads=[("s81", i)], writes=[("s81", i)])
            P.op("dve", lambda e, yy=yy, s8=s8: e.scalar_tensor_tensor(out=yy, in0=yy, scalar=s8[:, 1:2], in1=fw, op0=ALU.mult, op1=ALU.mult),
                 reads=[("yy", i), ("s81", i), "fw"], writes=[("yy", i)])
            P.dma("pool", y_out[tl * 128:(tl + 1) * 128, :], yy, reads=[("yy", i)], writes=[("yout", tl)])
            outs.append(("yout", tl))
        P.wait_all("sp", outs)
        P.emit()
    build_program.last_prog = P
    return nc


def _na_tables(rpb):
    cs = np.clip(np.arange(64) - 8, 0, 48)
    kc = np.arange(64)[:, None]
    c = np.arange(64)[None, :]
    colok = (kc >= cs[None, :]) & (kc <= cs[None, :] + 15)
    coff = np.clip(kc - c + 15, 0, 30)
    bg = np.full((8, 128, 640), NEG, np.float32)
    bf = np.full((8, 128, 896), NEG, np.float32)
    for jr in range(-1, 6):
        for jj in range(2):
            for ii in range(2):
                dk = 2 * jr + jj - ii - 4
                if abs(dk) > 7:
                    continue
                blk = np.where(colok[None], rpb[:, dk + 7][:, coff], np.float32(NEG)).astype(np.float32)
                bf[:, jj * 64:(jj + 1) * 64, (jr + 1) * 128 + ii * 64:(jr + 1) * 128 + (ii + 1) * 64] = blk
                if 0 <= jr <= 4 and -4 <= dk <= 3:
                    bg[:, jj * 64:(jj + 1) * 64, jr * 128 + ii * 64:jr * 128 + (ii + 1) * 64] = blk
    return bg, bf


def _core_masks(r0, rows):
    m = np.full((128, 12 * 128 + 4), NEG, np.float32)
    for idx, (b, t) in enumerate(SPECIAL):
        for jj in range(2):
            for ii in range(2):
                r = r0 + 2 * b + ii
                k = r0 + 2 * t + jj - 4
                rs = min(max(r - 4, 0), rows - 8)
                ok = (0 <= k < rows) and (rs <= k <= rs + 7)
                if ok:
                    m[jj * 64:(jj + 1) * 64, idx * 128 + ii * 64: idx * 128 + (ii + 1) * 64] = 0.0
    return m


def _prepare(x_prompt, x_sample, ln_w, w_in, na_rpb, ml_conv_w, ml_conv_b, ml_gate_b, ml_norm_w, w_out, final_norm_w):
    f = np.float32
    x_prompt = np.asarray(x_prompt, f)
    x_sample = np.asarray(x_sample, f)
    w_in0 = np.asarray(w_in, f)[0]
    wpad = np.zeros((D, NGRP * 128), f)
    wpad[:, :w_in0.shape[1]] = w_in0
    w_in_l = np.ascontiguousarray(wpad.reshape(16, 128, NGRP, 128).transpose(2, 1, 0, 3).reshape(NGRP, 128, 2048))
    w_out_l = np.ascontiguousarray(np.asarray(w_out, f)[0].reshape(16, 128, 2048).transpose(1, 0, 2).reshape(128, 16 * 2048))
    lnw_r = np.ascontiguousarray(np.tile(np.asarray(ln_w, f)[0][None], (128, 1)))
    fnw_r = np.ascontiguousarray(np.tile(np.asarray(final_norm_w, f)[None], (128, 1)))
    cwl = np.ascontiguousarray(np.asarray(ml_conv_w, f)[0].reshape(5, 16, 128).transpose(2, 1, 0).reshape(128, 80))
    cbl = np.ascontiguousarray(np.asarray(ml_conv_b, f)[0].reshape(16, 128).T)
    gbl = np.ascontiguousarray(np.tile(np.asarray(ml_gate_b, f)[0].reshape(1, 16), (128, 16)))
    mnwl = np.ascontiguousarray(np.asarray(ml_norm_w, f)[0].reshape(8, 128).T)
    ident = np.eye(128, dtype=f)
    tri = np.concatenate([np.triu(np.ones((128, 128), f)), np.tril(np.ones((128, 128), f))], axis=1)
    bg, bf = _na_tables(np.asarray(na_rpb, f)[0])

    in_maps = []
    for core in range(8):
        xe = np.zeros((2560, D), f)
        xo = np.zeros((3, 2048, D), f)
        xoh = np.zeros((3, 128, D), f)
        cfv = np.zeros((64,), f)
        if core < 4:
            seq = x_prompt[0]
            t0 = core * 2048
            r0, rows = core * 32, 128
            others = [i for i in range(4) if i != core]
            for o, i in enumerate(others):
                xo[o] = seq[i * 2048:(i + 1) * 2048]
                if i > 0:
                    xoh[o, 0:2] = seq[i * 2048 - 2:i * 2048]
                if i < 3:
                    xoh[o, 2:4] = seq[(i + 1) * 2048:(i + 1) * 2048 + 2]
                cfv[o] = 1.0 if i < core else 0.0
                cfv[3 + o] = 1.0 if i > core else 0.0
                for u, iu in enumerate(others):
                    cfv[6 + o * 3 + u] = 1.0 if i < iu < core else 0.0
                    cfv[15 + o * 3 + u] = 1.0 if core < iu < i else 0.0
        else:
            seq = x_sample[core - 4]
            t0 = 0
            r0, rows = 0, 32
        T = seq.shape[0]
        xe[0:2048] = seq[t0:t0 + 2048]
        if t0 - 256 >= 0:
            xe[2048:2304] = seq[t0 - 256:t0]
        if t0 + 2048 + 256 <= T:
            xe[2304:2560] = seq[t0 + 2048:t0 + 2304]
        in_maps.append({
            "xe": xe, "xo": xo, "xoh": xoh, "lnw": lnw_r, "fnw": fnw_r, "w_in": w_in_l, "w_out": w_out_l, "convw": cwl, "convb": cbl,
            "gateb": gbl, "mnw": mnwl, "ident": ident, "tri": tri, "bg": bg, "bf": bf,
            "msk": _core_masks(r0, rows), "cf": np.ascontiguousarray(np.tile(cfv[None], (128, 1))),
        })
    return in_maps


def kernel(x_prompt, x_sample, ln_w, w_in, na_rpb, ml_conv_w, ml_conv_b, ml_gate_b, ml_norm_w, w_out, final_norm_w):
    f = np.float32
    in_maps = _prepare(x_prompt, x_sample, ln_w, w_in, na_rpb, ml_conv_w, ml_conv_b, ml_gate_b, ml_norm_w, w_out, final_norm_w)
    nc = build_program()
    res = run_bass_kernel_spmd(nc, in_maps, core_ids=list(range(8)))
    ys = [np.asarray(res.results[c]["y"], f) for c in range(8)]
    y_prompt = np.concatenate(ys[0:4], axis=0)[None]
    y_sample = np.stack(ys[4:8], axis=0)
    return (y_prompt, y_sample)
```
